# Optimizing a Trainium2 kernel written in Bass

```python
import math
import jax, jax.numpy as jnp
from jax import lax
import numpy as np

D_MODEL = 2048
BATCH = 8
SEQ = 4096
DEPTH = 4

N_MEM = 256
HEAD_DIM = 128
MIX_WIDTH = D_MODEL
MEM_HEADS = 4
MEM_WIDTH = MEM_HEADS * HEAD_DIM
LOCAL_WIDTH = MIX_WIDTH - MEM_WIDTH
POOL_WINDOWS = (2, 4, 8, 16)
POOL_GROUP = LOCAL_WIDTH // len(POOL_WINDOWS)
DIL_PATTERNS = ((128, 1), (512, 4), (2048, 16))
DIL_HEADS_PER_GROUP = LOCAL_WIDTH // HEAD_DIM // len(DIL_PATTERNS)
DIL_GROUP_WIDTH = DIL_HEADS_PER_GROUP * HEAD_DIM
Q_BLOCK = 128
N_MIXERS = 2
N_POOL_LAYERS = (DEPTH + 1) // 2
N_DIL_LAYERS = DEPTH // 2
PEER_HEADS = 8
PEER_N_KEYS = 128
PEER_EXPERTS = PEER_N_KEYS * PEER_N_KEYS
PEER_TOPK = 16
PEER_QDIM = 256
PEER_HALF = PEER_QDIM // 2
PEER_CHUNK = 128
DEEPNORM_ALPHA = (2 * DEPTH) ** 0.25
DEEPNORM_BETA = (8 * DEPTH) ** -0.25
LN_EPS = 1e-5
NEG = -1e30

kernel_name = "hybrid_pool_dilated_peer_deepnorm"


def layer_norm(z, g, b):
    zf = z.astype(jnp.float32)
    mu = jnp.mean(zf, axis=-1, keepdims=True)
    var = jnp.mean(jnp.square(zf - mu), axis=-1, keepdims=True)
    return ((zf - mu) * lax.rsqrt(var + LN_EPS) * g.astype(jnp.float32)
            + b.astype(jnp.float32)).astype(z.dtype)


def pool_mixer(p, w_pool, s_pool):
    B, S, _ = p.shape
    pf = p.astype(jnp.float32)
    csum = jnp.cumsum(pf, axis=1)
    t = jnp.arange(S)
    outs = []
    for g, w in enumerate(POOL_WINDOWS):
        sl = slice(g * POOL_GROUP, (g + 1) * POOL_GROUP)
        cg = csum[..., sl]
        lower = jnp.pad(cg[:, :S - w], ((0, 0), (w, 0), (0, 0)))
        cnt = jnp.minimum(t + 1, w).astype(jnp.float32)[None, :, None]
        diff = ((cg - lower) / cnt - pf[..., sl]).astype(p.dtype)
        outs.append(diff @ w_pool[g])
    return jnp.concatenate(outs, axis=-1) * s_pool


def dilated_attention(q, k, v):
    B, S, _ = q.shape
    nb = S // Q_BLOCK
    starts = jnp.arange(nb) * Q_BLOCK
    scale = HEAD_DIM ** -0.5
    outs, lses = [], []
    for g, (window, dil) in enumerate(DIL_PATTERNS):
        sl = slice(g * DIL_GROUP_WIDTH, (g + 1) * DIL_GROUP_WIDTH)
        qg = q[..., sl].reshape(B, S, DIL_HEADS_PER_GROUP, HEAD_DIM)
        kg = k[..., sl].reshape(B, S, DIL_HEADS_PER_GROUP, HEAD_DIM)
        vg = v[..., sl].reshape(B, S, DIL_HEADS_PER_GROUP, HEAD_DIM)
        offs = dil * jnp.arange(window // dil + 1)

        def block(start, qg=qg, kg=kg, vg=vg, offs=offs):
            t = start + jnp.arange(Q_BLOCK)
            idx = t[:, None] - offs[None, :]
            valid = idx >= 0
            idx = jnp.maximum(idx, 0)
            qb = lax.dynamic_slice_in_dim(qg, start, Q_BLOCK, axis=1)
            kb = jnp.take(kg, idx, axis=1)
            vb = jnp.take(vg, idx, axis=1)
            s = jnp.einsum('bqhd,bqjhd->bhqj', qb, kb).astype(jnp.float32) * scale
            s = jnp.where(valid[None, None], s, NEG)
            lse = jax.nn.logsumexp(s, axis=-1)
            pr = jnp.exp(s - lse[..., None]).astype(vb.dtype)
            o = jnp.einsum('bhqj,bqjhd->bqhd', pr, vb)
            return o, lse

        o, lse = lax.map(block, starts)
        outs.append(o.transpose(1, 0, 2, 3, 4).reshape(B, S, DIL_HEADS_PER_GROUP, HEAD_DIM))
        lses.append(lse.transpose(1, 0, 3, 2).reshape(B, S, DIL_HEADS_PER_GROUP))
    alpha = jax.nn.softmax(jnp.stack(lses, axis=0), axis=0)
    merged = [outs[g] * alpha[g][..., None].astype(outs[g].dtype) for g in range(len(DIL_PATTERNS))]
    return jnp.concatenate(merged, axis=2).reshape(B, S, LOCAL_WIDTH)


def memory_attention(qm, mem, w_kv):
    B, S, _ = qm.shape
    kv = mem @ w_kv
    km = kv[..., :MEM_WIDTH].reshape(B, -1, MEM_HEADS, HEAD_DIM)
    vm = kv[..., MEM_WIDTH:].reshape(B, -1, MEM_HEADS, HEAD_DIM)
    qh = qm.reshape(B, S, MEM_HEADS, HEAD_DIM)
    s = jnp.einsum('bshd,bmhd->bhsm', qh, km).astype(jnp.float32) * (HEAD_DIM ** -0.5)
    pr = jax.nn.softmax(s, axis=-1).astype(vm.dtype)
    return jnp.einsum('bhsm,bmhd->bshd', pr, vm).reshape(B, S, MEM_WIDTH)


def peer_ffn(x, w_q, sub_keys, u, v):
    B, S, D = x.shape
    xt = x.reshape(-1, PEER_CHUNK, D)

    def chunk(xc):
        q = (xc @ w_q).reshape(PEER_CHUNK, PEER_HEADS, 2, PEER_HALF)
        s = jnp.einsum('chpk,hpnk->chpn', q, sub_keys).astype(jnp.float32)
        v1, i1 = lax.top_k(s[:, :, 0], PEER_TOPK)
        v2, i2 = lax.top_k(s[:, :, 1], PEER_TOPK)
        cand = (v1[..., :, None] + v2[..., None, :]).reshape(PEER_CHUNK, PEER_HEADS, PEER_TOPK * PEER_TOPK)
        cidx = (i1[..., :, None] * PEER_N_KEYS + i2[..., None, :]).reshape(PEER_CHUNK, PEER_HEADS, PEER_TOPK * PEER_TOPK)
        sv, sel = lax.top_k(cand, PEER_TOPK)
        eidx = jnp.take_along_axis(cidx, sel, axis=-1)
        gate = jax.nn.softmax(sv, axis=-1)
        u_sel = u[eidx]
        a = jnp.einsum('cd,chkd->chk', xc, u_sel).astype(jnp.float32)
        wgt = (gate * jax.nn.gelu(a, approximate=False)).astype(xc.dtype)
        v_sel = v[eidx]
        return jnp.einsum('chk,chkd->cd', wgt, v_sel)

    return lax.map(chunk, xt).reshape(B, S, D)


def setup_inputs(seed: int = 0) -> dict:
    key = jax.random.key(seed)
    ks = jax.random.split(key, 18)

    def nrm(k, shape, scale):
        return jax.random.normal(k, shape, jnp.float32) * scale

    s_in = D_MODEL ** -0.5
    beta = DEEPNORM_BETA
    x = nrm(ks[0], (BATCH, SEQ, D_MODEL), 1.0)
    mem = nrm(ks[1], (BATCH, N_MEM, D_MODEL), 1.0)
    w_in_a = jnp.concatenate([nrm(ks[2], (N_POOL_LAYERS, D_MODEL, LOCAL_WIDTH), s_in * beta),
                              nrm(ks[3], (N_POOL_LAYERS, D_MODEL, MEM_WIDTH), s_in)], axis=-1)
    w_pool = nrm(ks[4], (N_POOL_LAYERS, len(POOL_WINDOWS), POOL_GROUP, POOL_GROUP), POOL_GROUP ** -0.5)
    s_pool = 1.0 + nrm(ks[5], (N_POOL_LAYERS, LOCAL_WIDTH), 0.1)
    w_in_b = jnp.concatenate([nrm(ks[6], (N_DIL_LAYERS, D_MODEL, 2 * LOCAL_WIDTH), s_in),
                              nrm(ks[7], (N_DIL_LAYERS, D_MODEL, LOCAL_WIDTH), s_in * beta),
                              nrm(ks[8], (N_DIL_LAYERS, D_MODEL, MEM_WIDTH), s_in)], axis=-1)
    w_mem_kv = jnp.concatenate([nrm(ks[9], (DEPTH, D_MODEL, MEM_WIDTH), s_in),
                                nrm(ks[10], (DEPTH, D_MODEL, MEM_WIDTH), s_in * beta)], axis=-1)
    w_o = nrm(ks[11], (DEPTH, MIX_WIDTH, D_MODEL), MIX_WIDTH ** -0.5 * beta)
    ln_g = 1.0 + nrm(ks[12], (DEPTH, 2, D_MODEL), 0.05)
    ln_b = nrm(ks[13], (DEPTH, 2, D_MODEL), 0.02)
    peer_wq = nrm(ks[14], (DEPTH, D_MODEL, PEER_HEADS * PEER_QDIM), s_in)
    peer_keys = nrm(ks[15], (DEPTH, PEER_HEADS, 2, PEER_N_KEYS, PEER_HALF), PEER_HALF ** -0.5)
    peer_u = nrm(ks[16], (DEPTH, PEER_EXPERTS, D_MODEL), s_in)
    peer_v = nrm(ks[17], (DEPTH, PEER_EXPERTS, D_MODEL), beta * PEER_HEADS ** -0.5)
    return {"x": x, "mem": mem, "w_in_a": w_in_a, "w_pool": w_pool, "s_pool": s_pool,
            "w_in_b": w_in_b, "w_mem_kv": w_mem_kv, "w_o": w_o, "ln_g": ln_g, "ln_b": ln_b,
            "peer_wq": peer_wq, "peer_keys": peer_keys, "peer_u": peer_u, "peer_v": peer_v}


def reference(x, mem, w_in_a, w_pool, s_pool, w_in_b, w_mem_kv, w_o, ln_g, ln_b,
              peer_wq, peer_keys, peer_u, peer_v):
    ia = 0
    ib = 0
    for i in range(DEPTH):
        if i % N_MIXERS == 0:
            h = x @ w_in_a[ia]
            local = pool_mixer(h[..., :LOCAL_WIDTH], w_pool[ia], s_pool[ia])
            qm = h[..., LOCAL_WIDTH:]
            ia += 1
        else:
            h = x @ w_in_b[ib]
            q = h[..., :LOCAL_WIDTH]
            k = h[..., LOCAL_WIDTH:2 * LOCAL_WIDTH]
            v = h[..., 2 * LOCAL_WIDTH:3 * LOCAL_WIDTH]
            local = dilated_attention(q, k, v)
            qm = h[..., 3 * LOCAL_WIDTH:]
            ib += 1
        mem_out = memory_attention(qm, mem, w_mem_kv[i])
        mix = jnp.concatenate([local, mem_out], axis=-1) @ w_o[i]
        x = layer_norm(DEEPNORM_ALPHA * x + mix, ln_g[i, 0], ln_b[i, 0])
        ffn = peer_ffn(x, peer_wq[i], peer_keys[i], peer_u[i], peer_v[i])
        x = layer_norm(DEEPNORM_ALPHA * x + ffn, ln_g[i, 1], ln_b[i, 1])
    return x
```

```python
import math
from contextlib import ExitStack

import numpy as np

import concourse.bass as bass
import concourse.mybir as mybir
from concourse.bass_utils import run_bass_kernel_spmd

F32 = mybir.dt.float32
BF16 = mybir.dt.bfloat16
AF = mybir.ActivationFunctionType
ALU = mybir.AluOpType

T = 4096
D = 2048
NT = T // 128
NBLK = T // 512
DEPTH = 4
LOCAL = 1536
ALPHA = (2 * DEPTH) ** 0.25
EPS = 1e-5
ISQ = 128 ** -0.5
NEXP = 16384
EG = 512
SAME_ENGINE_SYNC = True
NG = NEXP // EG


class Sem:
    __slots__ = ("h", "count")

    def __init__(self, h):
        self.h = h
        self.count = 0


class Buf:
    __slots__ = ("w", "r", "accum", "dsem", "name")

    def __init__(self, name="", accum=False):
        self.w = {}
        self.r = {}
        self.accum = accum
        self.dsem = None
        self.name = name


class Tile:
    __slots__ = ("t", "b")

    def __init__(self, t, b):
        self.t = t
        self.b = b


class Prog:
    ENGS = ("sp", "pe", "act", "dve", "pool")
    ATTR = {"sp": "sync", "pe": "tensor", "act": "scalar", "dve": "vector", "pool": "gpsimd"}

    def __init__(self, nc, es, n_dma_sems=40):
        self.nc = nc
        self.esem = {e: Sem(es.enter_context(nc.semaphore("se_" + e))) for e in ("pe", "act", "dve", "pool")}
        self.dpool = [Sem(es.enter_context(nc.semaphore("sd%d" % i))) for i in range(n_dma_sems)]
        self.dnext = 0
        self.ops = {e: [] for e in self.ENGS}
        self.seen = {e: {} for e in self.ENGS}
        self.nps = 0
        self.uid = 0

    def tile(self, es, name, shape, dt):
        self.uid += 1
        t = es.enter_context(self.nc.sbuf_tensor("%s_%d" % (name, self.uid), list(shape), dt))
        return Tile(t, Buf(name))

    def psum(self, es, name, shape, dt):
        self.uid += 1
        t = es.enter_context(self.nc.psum_tensor("%s_%d" % (name, self.uid), list(shape), dt))
        return Tile(t, Buf(name))

    def _dsem(self, buf):
        if buf.dsem is None:
            assert self.dnext < len(self.dpool), "out of DMA semaphores"
            buf.dsem = self.dpool[self.dnext]
            self.dnext += 1
        return buf.dsem

    def _record(self, eng, fns, reads, writes, sem, inc_each):
        deps = {}

        def merge(d):
            for s, v in d.items():
                if deps.get(s, 0) < v:
                    deps[s] = v

        for b in reads:
            merge(b.w)
        for b in writes:
            merge(b.r)
            if not b.accum:
                merge(b.w)
        n = len(fns)
        if inc_each:
            sem.count += 16 * n
        else:
            sem.count += 1
        val = sem.count
        seen = self.seen[eng]
        waits = []
        own = self.esem.get(eng)
        for s, v in deps.items():
            if s is own and (eng == "pe" or not SAME_ENGINE_SYNC):
                continue
            if seen.get(s, 0) >= v:
                continue
            seen[s] = v
            waits.append((s, v))
        self.ops[eng].append((waits, fns, sem, inc_each))
        for b in reads:
            if b.r.get(sem, 0) < val:
                b.r[sem] = val
        for b in writes:
            if b.accum:
                if b.w.get(sem, 0) < val:
                    b.w[sem] = val
            else:
                b.w = {sem: val}
                b.r = {}

    def compute(self, eng, fns, reads=(), writes=()):
        if not isinstance(fns, (list, tuple)):
            fns = [fns]
        self._record(eng, list(fns), reads, writes, self.esem[eng], False)

    def dma(self, eng, pairs, sbuf, reads=(), writes=()):
        sem = self._dsem(sbuf)
        fns = [(lambda e, o=o, i=i: e.dma_start(out=o, in_=i)) for (o, i) in pairs]
        self._record(eng, fns, reads, writes, sem, True)

    def flush(self):
        waits = []
        seen = self.seen["sp"]
        for s in self.dpool[: self.dnext]:
            if s.count > 0 and seen.get(s, 0) < s.count:
                seen[s] = s.count
                waits.append((s, s.count))
        self.ops["sp"].append((waits, [], None, False))
        nc = self.nc
        with nc.Block() as block:
            for eng in self.ENGS:
                ops = self.ops[eng]

                def body(e, ops=ops):
                    for waits, fns, sem, inc_each in ops:
                        for s, v in waits:
                            e.wait_ge(s.h, v)
                        n = len(fns)
                        for i, f in enumerate(fns):
                            ins = f(e)
                            if inc_each:
                                ins.then_inc(sem.h, 16)
                            elif i == n - 1:
                                ins.then_inc(sem.h, 1)

                getattr(block, self.ATTR[eng])(body)
        self.nps += sum(len(o[1]) for e in self.ENGS for o in self.ops[e])
        self.ops = {e: [] for e in self.ENGS}
        self.dnext = 0


class PsumRing:
    def __init__(self, tiles):
        self.tiles = tiles
        self.i = 0

    def next(self):
        t = self.tiles[self.i % len(self.tiles)]
        self.i += 1
        return t


def load_w_cast(p, dst, dram_rows_ap, kchunks, sub=4):
    v = dram_rows_ap.rearrange("(k q) c -> q k c", q=128)
    pairs = []
    step = max(1, kchunks // sub)
    for k0 in range(0, kchunks, step):
        pairs.append((dst.t[:, k0:k0 + step, :], v[:, k0:k0 + step, :]))
    p.dma("pool", pairs, dst.b, writes=[dst.b])


def emit_ln_tile(p, S, r, xn, lng, lnb):
    mv, rstd, eps, junk = S["ln_mv"], S["ln_rstd"], S["eps"], S["junk"]
    p.compute("dve", lambda e: e.reduce_sum(out=mv.t[:, 0:1], in_=r.t[:, :], axis=mybir.AxisListType.X),
              reads=[r.b], writes=[mv.b])
    p.compute("dve", lambda e: e.tensor_scalar(out=mv.t[:, 0:1], in0=mv.t[:, 0:1], scalar1=-1.0 / D, scalar2=None, op0=ALU.mult),
              reads=[mv.b], writes=[mv.b])
    p.compute("act", lambda e: e.activation(out=junk.t[:, :], in_=r.t[:, :], func=AF.Square, bias=mv.t[:, 0:1], scale=1.0,
                                            accum_out=rstd.t[:, 0:1]), reads=[r.b, mv.b], writes=[junk.b, rstd.b])
    p.compute("act", lambda e: e.activation(out=rstd.t[:, 0:1], in_=rstd.t[:, 0:1], func=AF.Ln, bias=eps.t[:, 0:1], scale=1.0 / D),
              reads=[rstd.b, eps.b], writes=[rstd.b])
    p.compute("act", lambda e: e.activation(out=rstd.t[:, 0:1], in_=rstd.t[:, 0:1], func=AF.Exp, scale=-0.5),
              reads=[rstd.b], writes=[rstd.b])
    p.compute("dve", lambda e: e.tensor_tensor(out=mv.t[:, 1:2], in0=mv.t[:, 0:1], in1=rstd.t[:, 0:1], op=ALU.mult),
              reads=[mv.b, rstd.b], writes=[mv.b])
    p.compute("act", lambda e: e.activation(out=xn.t[:, :], in_=r.t[:, :], func=AF.Identity, bias=mv.t[:, 1:2], scale=rstd.t[:, 0:1]),
              reads=[r.b, mv.b, rstd.b], writes=[xn.b])
    p.compute("dve", lambda e: e.tensor_tensor(out=xn.t[:, :], in0=xn.t[:, :], in1=lng.t[:, :], op=ALU.mult),
              reads=[xn.b, lng.b], writes=[xn.b])
    p.compute("dve", lambda e: e.tensor_tensor(out=xn.t[:, :], in0=xn.t[:, :], in1=lnb.t[:, :], op=ALU.add),
              reads=[xn.b, lnb.b], writes=[xn.b])


def emit_transpose_tile(p, S, src, xTblk, tt, ident, pst):
    xb = S["xb"]
    p.compute("act", lambda e: e.copy(out=xb.t[:, :], in_=src.t[:, :]), reads=[src.b], writes=[xb.b])
    for half in range(2):
        ps = pst.next()
        fns = [(lambda e, c=c, ps=ps: e.transpose(ps.t[:, (c % 8) * 128:(c % 8 + 1) * 128],
                                                   xb.t[:, c * 128:(c + 1) * 128], ident.t[:, :]))
               for c in range(half * 8, half * 8 + 8)]
        p.compute("pe", fns, reads=[xb.b, ident.b], writes=[ps.b])
        p.compute("act", lambda e, ps=ps, half=half: e.copy(
            out=xTblk.t[:, half * 8:half * 8 + 8, tt * 128:(tt + 1) * 128],
            in_=ps.t[:, :].rearrange("q (c t) -> q c t", c=8)), reads=[ps.b], writes=[xTblk.b])


def xT_view(XT):
    return XT.rearrange("(c q) t -> q c t", q=128)


def phase_init(p, G):
    nc = p.nc
    with ExitStack() as es:
        ident = p.tile(es, "ident", [128, 128], BF16)
        p.dma("pool", [(ident.t[:, :], G["ident"])], ident.b, writes=[ident.b])
        S = {"xb": p.tile(es, "xb", [128, D], BF16)}
        pst = PsumRing([p.psum(es, "pst", [128, 1024], BF16) for _ in range(4)])
        xin = [p.tile(es, "xin", [128, D], F32) for _ in range(3)]
        xTb = [p.tile(es, "xTb", [128, 16, 512], BF16) for _ in range(2)]
        memT = p.tile(es, "memT", [128, 16, 256], BF16)
        for mt in range(2):
            xt = xin[mt % 3]
            p.dma("sp", [(xt.t[:, :], G["mem"][mt * 128:(mt + 1) * 128, :])], xt.b, writes=[xt.b])
            xb = S["xb"]
            p.compute("act", lambda e, xt=xt: e.copy(out=xb.t[:, :], in_=xt.t[:, :]), reads=[xt.b], writes=[xb.b])
            for half in range(2):
                ps = pst.next()
                fns = [(lambda e, c=c, ps=ps: e.transpose(ps.t[:, (c % 8) * 128:(c % 8 + 1) * 128],
                                                           xb.t[:, c * 128:(c + 1) * 128], ident.t[:, :]))
                       for c in range(half * 8, half * 8 + 8)]
                p.compute("pe", fns, reads=[xb.b, ident.b], writes=[ps.b])
                p.compute("act", lambda e, ps=ps, half=half, mt=mt: e.copy(
                    out=memT.t[:, half * 8:half * 8 + 8, mt * 128:(mt + 1) * 128],
                    in_=ps.t[:, :].rearrange("q (c t) -> q c t", c=8)), reads=[ps.b], writes=[memT.b])
        p.dma("sp", [(xT_view(G["MEMT"]), memT.t[:, :, :])], memT.b, reads=[memT.b], writes=[G["MEMT_b"]])
        for bi in range(NBLK):
            blk = xTb[bi % 2]
            for tt in range(4):
                ti = bi * 4 + tt
                xt = xin[ti % 3]
                p.dma("sp", [(xt.t[:, :], G["x"][ti * 128:(ti + 1) * 128, :])], xt.b, writes=[xt.b])
                emit_transpose_tile(p, S, xt, blk, tt, ident, pst)
            p.dma("sp", [(xT_view(G["XT"][0])[:, :, bi * 512:(bi + 1) * 512], blk.t[:, :, :])], blk.b,
                  reads=[blk.b], writes=[G["XT_b"][0]])
        p.flush()


def emit_mem_kv(p, es, wkv_dram, G):
    kmT = p.tile(es, "kmT", [128, 4, 256], BF16)
    vm = p.tile(es, "vm", [128, 2, 512], BF16)
    with ExitStack() as es2:
        psr = PsumRing([p.psum(es2, "psk", [128, 512], F32) for _ in range(4)])
        wkv = p.tile(es2, "wkv", [128, 16, 1024], BF16)
        memT = p.tile(es2, "memT", [128, 16, 256], BF16)
        load_w_cast(p, wkv, wkv_dram, 16)
        p.dma("sp", [(memT.t[:, :, :], xT_view(G["MEMT"]))], memT.b, reads=[G["MEMT_b"]], writes=[memT.b])
        for h in range(4):
            ps = psr.next()
            fns = [(lambda e, k=k, h=h, ps=ps: e.matmul(ps.t[:, 0:256], lhsT=wkv.t[:, k, h * 128:(h + 1) * 128],
                                                        rhs=memT.t[:, k, :], start=(k == 0), stop=(k == 15)))
                   for k in range(16)]
            p.compute("pe", fns, reads=[wkv.b, memT.b], writes=[ps.b])
            p.compute("act", lambda e, h=h, ps=ps: e.copy(out=kmT.t[:, h, :], in_=ps.t[:, 0:256]), reads=[ps.b], writes=[kmT.b])
        for mc in range(2):
            ps = psr.next()
            fns = [(lambda e, k=k, mc=mc, ps=ps: e.matmul(ps.t[:, :], lhsT=memT.t[:, k, mc * 128:(mc + 1) * 128],
                                                          rhs=wkv.t[:, k, 512:1024], start=(k == 0), stop=(k == 15)))
                   for k in range(16)]
            p.compute("pe", fns, reads=[wkv.b, memT.b], writes=[ps.b])
            p.compute("act", lambda e, mc=mc, ps=ps: e.copy(out=vm.t[:, mc, :], in_=ps.t[:, :]), reads=[ps.b], writes=[vm.b])
        p.flush()
    return kmT, vm


def emit_mem_attn_block(p, S, qmT, kmT, vm, ones, psr, mo):
    expT, rz = S["expT"], S["rz"]
    for h in range(4):
        pss = [psr.next(), psr.next()]
        for mc in range(2):
            p.compute("pe", lambda e, h=h, mc=mc, ps=pss[mc]: e.matmul(
                ps.t[:, :], lhsT=kmT.t[:, h, mc * 128:(mc + 1) * 128], rhs=qmT.t[:, h, :], start=True, stop=True),
                reads=[kmT.b, qmT.b], writes=[pss[mc].b])
            p.compute("act", lambda e, mc=mc, ps=pss[mc]: e.activation(out=expT.t[:, mc, :], in_=ps.t[:, :], func=AF.Exp, scale=ISQ),
                      reads=[pss[mc].b], writes=[expT.b])
        psn, psz = psr.next(), psr.next()
        p.compute("pe", [(lambda e, h=h, mc=mc, ps=psn: e.matmul(ps.t[:, :], lhsT=vm.t[:, mc, h * 128:(h + 1) * 128],
                                                               rhs=expT.t[:, mc, :], start=(mc == 0), stop=(mc == 1)))
                         for mc in range(2)], reads=[vm.b, expT.b], writes=[psn.b])
        p.compute("pe", [(lambda e, mc=mc, ps=psz: e.matmul(ps.t[:, :], lhsT=ones.t[:, :], rhs=expT.t[:, mc, :],
                                                          start=(mc == 0), stop=(mc == 1)))
                         for mc in range(2)], reads=[ones.b, expT.b], writes=[psz.b])
        p.compute("dve", lambda e, ps=psz: e.reciprocal(out=rz.t[:, :], in_=ps.t[:, :]), reads=[psz.b], writes=[rz.b])
        p.compute("dve", lambda e, h=h, ps=psn: e.tensor_tensor(out=mo.t[:, h, :], in0=ps.t[:, :], in1=rz.t[:, :], op=ALU.mult),
                  reads=[psn.b, rz.b], writes=[mo.b])


def phase_A1(p, G, l, ia, cur):
    nc = p.nc
    with ExitStack() as es:
        kmT, vm = emit_mem_kv(p, es, G["w_mem_kv"][l], G)
        psr = PsumRing([p.psum(es, "ps", [128, 512], F32) for _ in range(8)])
        ones = p.tile(es, "ones", [128, 128], BF16)
        p.dma("pool", [(ones.t[:, :], G["ones"])], ones.b, writes=[ones.b])
        win = p.tile(es, "win", [128, 16, 2048], BF16)
        load_w_cast(p, win, G["w_in_a"][ia], 16, sub=8)
        wpool = p.tile(es, "wpool", [128, 4, 3, 384], BF16)
        p.dma("pool", [(wpool.t[:, g, :, :], G["w_pool"][ia, g].rearrange("(k q) c -> q k c", q=128)) for g in range(4)],
              wpool.b, writes=[wpool.b])
        spool = p.tile(es, "spool", [128, 12], F32)
        p.dma("sp", [(spool.t[:, :], G["s_pool_l"][ia])], spool.b, writes=[spool.b])
        rc = p.tile(es, "rc", [128, 4, 512], F32)
        p.dma("sp", [(rc.t[:, :, :], G["rc"])], rc.b, writes=[rc.b])
        xTb = [p.tile(es, "xTb", [128, 16, 512], BF16) for _ in range(2)]
        pext = p.tile(es, "pext", [128, 12, 528], F32)
        pg = [Buf("pg%d" % g) for g in range(4)]
        tA = p.tile(es, "tA", [128, 3, 528], F32)
        tB = p.tile(es, "tB", [128, 3, 528], F32)
        diff = [p.tile(es, "diff", [128, 3, 512], BF16) for _ in range(2)]
        loc = [p.tile(es, "loc", [128, 3, 512], BF16) for _ in range(2)]
        qmT = p.tile(es, "qmT", [128, 4, 512], BF16)
        mo = [p.tile(es, "mo", [128, 4, 512], BF16) for _ in range(2)]
        S = {"expT": p.tile(es, "expT", [128, 2, 512], BF16), "rz": p.tile(es, "rz", [128, 512], F32)}
        p.compute("dve", lambda e: e.memset(pext.t[:, :, 0:16], 0.0), writes=pg)
        XTv = xT_view(G["XT"][cur])
        MIXv = xT_view(G["MIXT"])
        def load_xTb(bj):
            x_ = xTb[bj % 2]
            p.dma("sp", [(x_.t[:, :, :], XTv[:, :, bj * 512:(bj + 1) * 512])], x_.b, reads=[G["XT_b"][cur]], writes=[x_.b])

        load_xTb(0)
        for bi in range(NBLK):
            xT = xTb[bi % 2]
            if bi + 1 < NBLK:
                load_xTb(bi + 1)
            for j in range(16):
                ps = psr.next()
                fns = [(lambda e, k=k, j=j, ps=ps, xT=xT: e.matmul(ps.t[:, :], lhsT=win.t[:, k, j * 128:(j + 1) * 128],
                                                                   rhs=xT.t[:, k, :], start=(k == 0), stop=(k == 15)))
                       for k in range(16)]
                p.compute("pe", fns, reads=[win.b, xT.b], writes=[ps.b])
                if j < 12:
                    p.compute("act", lambda e, j=j, ps=ps: e.copy(out=pext.t[:, j, 16:528], in_=ps.t[:, :]),
                              reads=[ps.b], writes=[pg[j // 3]])
                else:
                    p.compute("act", lambda e, j=j, ps=ps: e.copy(out=qmT.t[:, j - 12, :], in_=ps.t[:, :]),
                              reads=[ps.b], writes=[qmT.b])
            mob = mo[bi % 2]
            emit_mem_attn_block(p, S, qmT, kmT, vm, ones, psr, mob)
            p.dma("sp", [(MIXv[:, 12:16, bi * 512:(bi + 1) * 512], mob.t[:, :, :])], mob.b, reads=[mob.b], writes=[G["MIXT_b"]])
            for g in range(4):
                w = 2 << g
                src = pext.t[:, 3 * g:3 * g + 3, :]
                lv = [tA, tB, tA, tB]
                off = 0
                prev = None
                for s in range(g + 1):
                    step = 1 << s
                    off += step
                    dst = lv[s]
                    if prev is None:
                        a0 = src[:, :, off:528]
                        a1 = src[:, :, off - step:528 - step]
                        rb = [pg[g]]
                    else:
                        a0 = prev.t[:, :, off:528]
                        a1 = prev.t[:, :, off - step:528 - step]
                        rb = [prev.b]
                    p.compute("dve", lambda e, dst=dst, a0=a0, a1=a1, off=off: e.tensor_tensor(
                        out=dst.t[:, :, off:528], in0=a0, in1=a1, op=ALU.add), reads=rb, writes=[dst.b])
                    prev = dst
                fin = prev
                df = diff[g % 2]
                if bi == 0:
                    p.compute("dve", lambda e, fin=fin, g=g: e.tensor_tensor(
                        out=fin.t[:, :, 16:528], in0=fin.t[:, :, 16:528],
                        in1=rc.t[:, g:g + 1, :].to_broadcast([128, 3, 512]), op=ALU.mult),
                        reads=[fin.b, rc.b], writes=[fin.b])
                    p.compute("dve", lambda e, fin=fin, df=df, src=src: e.tensor_tensor(
                        out=df.t[:, :, :], in0=fin.t[:, :, 16:528], in1=src[:, :, 16:528], op=ALU.subtract),
                        reads=[fin.b, pg[g]], writes=[df.b])
                else:
                    p.compute("dve", lambda e, fin=fin, df=df, src=src, w=w: e.scalar_tensor_tensor(
                        out=df.t[:, :, :], in0=fin.t[:, :, 16:528], scalar=1.0 / w, in1=src[:, :, 16:528],
                        op0=ALU.mult, op1=ALU.subtract), reads=[fin.b, pg[g]], writes=[df.b])
                p.compute("dve", lambda e, g=g: e.tensor_copy(out=pext.t[:, 3 * g:3 * g + 3, 0:16],
                                                               in_=pext.t[:, 3 * g:3 * g + 3, 512:528]),
                          reads=[pg[g]], writes=[pg[g]])
                lc = loc[g % 2]
                for oc in range(3):
                    ps = psr.next()
                    fns = [(lambda e, kc=kc, oc=oc, ps=ps, g=g, df=df: e.matmul(
                        ps.t[:, :], lhsT=wpool.t[:, g, kc, oc * 128:(oc + 1) * 128], rhs=df.t[:, kc, :],
                        start=(kc == 0), stop=(kc == 2))) for kc in range(3)]
                    p.compute("pe", fns, reads=[wpool.b, df.b], writes=[ps.b])
                    p.compute("act", lambda e, oc=oc, ps=ps, g=g, lc=lc: e.activation(
                        out=lc.t[:, oc, :], in_=ps.t[:, :], func=AF.Copy, scale=spool.t[:, 3 * g + oc:3 * g + oc + 1]),
                        reads=[ps.b, spool.b], writes=[lc.b])
                p.dma("sp", [(MIXv[:, 3 * g:3 * g + 3, bi * 512:(bi + 1) * 512], lc.t[:, :, :])], lc.b,
                      reads=[lc.b], writes=[G["MIXT_b"]])
        p.flush()


def phase_O(p, G, l, cur, which=0):
    nxt = 1 - cur
    with ExitStack() as es:
        psr = PsumRing([p.psum(es, "ps", [128, 512], F32) for _ in range(6)])
        pst = PsumRing([p.psum(es, "pst", [128, 1024], BF16) for _ in range(2)])
        ident = p.tile(es, "ident", [128, 128], BF16)
        p.dma("pool", [(ident.t[:, :], G["ident"])], ident.b, writes=[ident.b])
        if which == 0:
            wo = p.tile(es, "wo", [128, 16, 2048], BF16)
            load_w_cast(p, wo, G["w_o"][l], 16, sub=8)
            mixb = [p.tile(es, "mixb", [128, 16, 512], BF16) for _ in range(2)]
        else:
            yin = [p.tile(es, "yin", [128, D], F32) for _ in range(2)]
        lng = p.tile(es, "lng", [128, D], F32)
        lnb = p.tile(es, "lnb", [128, D], F32)
        p.dma("sp", [(lng.t[:, :], G["ln_g"][l, which:which + 1, :].to_broadcast([128, D]))], lng.b, writes=[lng.b])
        p.dma("sp", [(lnb.t[:, :], G["ln_b"][l, which:which + 1, :].to_broadcast([128, D]))], lnb.b, writes=[lnb.b])
        S = {"xb": p.tile(es, "xb", [128, D], BF16), "junk": p.tile(es, "junk", [128, D], BF16),
             "ln_mv": p.tile(es, "lnmv", [128, 2], F32), "ln_rstd": p.tile(es, "lnrs", [128, 1], F32),
             "eps": p.tile(es, "eps", [128, 1], F32)}
        p.compute("dve", lambda e: e.memset(S["eps"].t[:, :], EPS), writes=[S["eps"].b])
        xin = [p.tile(es, "xin", [128, D], F32) for _ in range(2)]
        rr = [p.tile(es, "rr", [128, D], F32) for _ in range(2)]
        xn = [p.tile(es, "xn", [128, D], F32) for _ in range(2)]
        xTo = [p.tile(es, "xTo", [128, 16, 512], BF16) for _ in range(2)]
        MIXv = xT_view(G["MIXT"])

        def load_xy(tj):
            x_ = xin[tj % 2]
            p.dma("sp", [(x_.t[:, :], G["X"][cur][tj * 128:(tj + 1) * 128, :])], x_.b, reads=[G["X_b"][cur]], writes=[x_.b])
            if which == 1:
                y_ = yin[tj % 2]
                p.dma("sp", [(y_.t[:, :], G["Y"][tj * 128:(tj + 1) * 128, :])], y_.b, reads=[G["Y_b"]], writes=[y_.b])

        for bi in range(NBLK):
            if which == 0:
                mb = mixb[bi % 2]
                if bi == 0:
                    p.dma("sp", [(mb.t[:, :, :], MIXv[:, :, 0:512])], mb.b, reads=[G["MIXT_b"]], writes=[mb.b])
                if bi + 1 < NBLK:
                    mb2 = mixb[(bi + 1) % 2]
                    p.dma("sp", [(mb2.t[:, :, :], MIXv[:, :, (bi + 1) * 512:(bi + 2) * 512])], mb2.b, reads=[G["MIXT_b"]], writes=[mb2.b])
            xo = xTo[bi % 2]
            for tt in range(4):
                ti = bi * 4 + tt
                xt = xin[ti % 2]
                if ti == 0:
                    load_xy(0)
                if ti + 1 < NT:
                    load_xy(ti + 1)
                r = rr[ti % 2]
                if which == 1:
                    yt = yin[ti % 2]
                    p.compute("dve", lambda e, xt=xt, r=r, yt=yt: e.scalar_tensor_tensor(
                        out=r.t[:, :], in0=xt.t[:, :], scalar=ALPHA, in1=yt.t[:, :], op0=ALU.mult, op1=ALU.add),
                        reads=[yt.b, xt.b], writes=[r.b])
                for cb in (range(4) if which == 0 else ()):
                    ps = psr.next()
                    fns = [(lambda e, k=k, cb=cb, ps=ps, tt=tt, mb=mb: e.matmul(
                        ps.t[:, :], lhsT=mb.t[:, k, tt * 128:(tt + 1) * 128], rhs=wo.t[:, k, cb * 512:(cb + 1) * 512],
                        start=(k == 0), stop=(k == 15))) for k in range(16)]
                    p.compute("pe", fns, reads=[wo.b, mb.b], writes=[ps.b])
                    p.compute("dve", lambda e, cb=cb, ps=ps, xt=xt, r=r: e.scalar_tensor_tensor(
                        out=r.t[:, cb * 512:(cb + 1) * 512], in0=xt.t[:, cb * 512:(cb + 1) * 512], scalar=ALPHA,
                        in1=ps.t[:, :], op0=ALU.mult, op1=ALU.add), reads=[ps.b, xt.b], writes=[r.b])
                x2 = xn[ti % 2]
                emit_ln_tile(p, S, r, x2, lng, lnb)
                p.dma("sp", [(G["X"][nxt][ti * 128:(ti + 1) * 128, :], x2.t[:, :])], x2.b, reads=[x2.b], writes=[G["X_b"][nxt]])
                emit_transpose_tile(p, S, x2, xo, tt, ident, pst)
            p.dma("sp", [(xT_view(G["XT"][nxt])[:, :, bi * 512:(bi + 1) * 512], xo.t[:, :, :])], xo.b,
                  reads=[xo.b], writes=[G["XT_b"][nxt]])
        p.flush()


def phase_B1(p, G, l, ib, cur):
    XTv = xT_view(G["XT"][cur])
    wsrc = G["w_in_b"][ib]
    with ExitStack() as es:
        psr = PsumRing([p.psum(es, "ps", [128, 512], F32) for _ in range(8)])
        w = p.tile(es, "wqk", [128, 16, 3072], BF16)
        load_w_cast(p, w, wsrc[:, 0:3072], 16, sub=16)
        xTb = [p.tile(es, "xTb", [128, 16, 512], BF16) for _ in range(2)]
        ob = [p.tile(es, "ob", [128, 24, 512], BF16) for _ in range(2)]
        QKv = xT_view(G["QKT"])
        def load_xTb(bj):
            x_ = xTb[bj % 2]
            p.dma("sp", [(x_.t[:, :, :], XTv[:, :, bj * 512:(bj + 1) * 512])], x_.b, reads=[G["XT_b"][cur]], writes=[x_.b])

        load_xTb(0)
        for bi in range(NBLK):
            xT = xTb[bi % 2]
            if bi + 1 < NBLK:
                load_xTb(bi + 1)
            o_ = ob[bi % 2]
            for j in range(24):
                ps = psr.next()
                fns = [(lambda e, k=k, j=j, ps=ps, xT=xT: e.matmul(ps.t[:, :], lhsT=w.t[:, k, j * 128:(j + 1) * 128],
                                                                   rhs=xT.t[:, k, :], start=(k == 0), stop=(k == 15)))
                       for k in range(16)]
                p.compute("pe", fns, reads=[w.b, xT.b], writes=[ps.b])
                if j % 2 == 0:
                    p.compute("act", lambda e, j=j, ps=ps, o_=o_: e.copy(out=o_.t[:, j, :], in_=ps.t[:, :]), reads=[ps.b], writes=[o_.b])
                else:
                    p.compute("dve", lambda e, j=j, ps=ps, o_=o_: e.tensor_copy(out=o_.t[:, j, :], in_=ps.t[:, :]), reads=[ps.b], writes=[o_.b])
            p.dma("sp", [(QKv[:, 0:12, bi * 512:(bi + 1) * 512], o_.t[:, 0:12, :]), (QKv[:, 12:24, bi * 512:(bi + 1) * 512], o_.t[:, 12:24, :])],
                  o_.b, reads=[o_.b], writes=[G["QKT_b"]])
        p.flush()
    with ExitStack() as es:
        kmT, vm = emit_mem_kv(p, es, G["w_mem_kv"][l], G)
        psr = PsumRing([p.psum(es, "ps", [128, 512], F32) for _ in range(8)])
        ones = p.tile(es, "ones", [128, 128], BF16)
        p.dma("pool", [(ones.t[:, :], G["ones"])], ones.b, writes=[ones.b])
        w = p.tile(es, "wvq", [128, 16, 2048], BF16)
        load_w_cast(p, w, wsrc[:, 3072:5120], 16, sub=16)
        xTb = [p.tile(es, "xTb", [128, 16, 512], BF16) for _ in range(2)]
        vst = [p.tile(es, "vst", [128, 1536], BF16) for _ in range(2)]
        qmT = p.tile(es, "qmT", [128, 4, 512], BF16)
        mo = [p.tile(es, "mo", [128, 4, 512], BF16) for _ in range(2)]
        S = {"expT": p.tile(es, "expT", [128, 2, 512], BF16), "rz": p.tile(es, "rz", [128, 512], F32)}
        MIXv = xT_view(G["MIXT"])
        def load_xTb(bj):
            x_ = xTb[bj % 2]
            p.dma("sp", [(x_.t[:, :, :], XTv[:, :, bj * 512:(bj + 1) * 512])], x_.b, reads=[G["XT_b"][cur]], writes=[x_.b])

        load_xTb(0)
        for bi in range(NBLK):
            xT = xTb[bi % 2]
            if bi + 1 < NBLK:
                load_xTb(bi + 1)
            for j in range(4):
                ps = psr.next()
                fns = [(lambda e, k=k, j=j, ps=ps, xT=xT: e.matmul(ps.t[:, :], lhsT=w.t[:, k, 1536 + j * 128:1536 + (j + 1) * 128],
                                                                   rhs=xT.t[:, k, :], start=(k == 0), stop=(k == 15)))
                       for k in range(16)]
                p.compute("pe", fns, reads=[w.b, xT.b], writes=[ps.b])
                p.compute("act", lambda e, j=j, ps=ps: e.copy(out=qmT.t[:, j, :], in_=ps.t[:, :]), reads=[ps.b], writes=[qmT.b])
            mob = mo[bi % 2]
            emit_mem_attn_block(p, S, qmT, kmT, vm, ones, psr, mob)
            p.dma("sp", [(MIXv[:, 12:16, bi * 512:(bi + 1) * 512], mob.t[:, :, :])], mob.b, reads=[mob.b], writes=[G["MIXT_b"]])
            for tt in range(4):
                ti = bi * 4 + tt
                vs = vst[ti % 2]
                for cb in range(3):
                    ps = psr.next()
                    fns = [(lambda e, k=k, cb=cb, ps=ps, xT=xT, tt=tt: e.matmul(
                        ps.t[:, :], lhsT=xT.t[:, k, tt * 128:(tt + 1) * 128], rhs=w.t[:, k, cb * 512:(cb + 1) * 512],
                        start=(k == 0), stop=(k == 15))) for k in range(16)]
                    p.compute("pe", fns, reads=[w.b, xT.b], writes=[ps.b])
                    if cb % 2 == 0:
                        p.compute("act", lambda e, cb=cb, ps=ps, vs=vs: e.copy(out=vs.t[:, cb * 512:(cb + 1) * 512], in_=ps.t[:, :]),
                                  reads=[ps.b], writes=[vs.b])
                    else:
                        p.compute("dve", lambda e, cb=cb, ps=ps, vs=vs: e.tensor_copy(out=vs.t[:, cb * 512:(cb + 1) * 512], in_=ps.t[:, :]),
                                  reads=[ps.b], writes=[vs.b])
                p.dma("sp", [(G["V"][ti * 128:(ti + 1) * 128, :], vs.t[:, :])], vs.b, reads=[vs.b], writes=[G["V_b"]])
        p.flush()


def phase_B2(p, G, l, cur):
    DILS = (1, 4, 16)
    with ExitStack() as es:
        psS = [p.psum(es, "psS", [128, 256], F32) for _ in range(2)]
        psN = [p.psum(es, "psN", [128, 128], F32) for _ in range(2)]
        psZ = [p.psum(es, "psZ", [128, 128], F32) for _ in range(2)]
        ones = p.tile(es, "ones", [128, 128], BF16)
        p.dma("pool", [(ones.t[:, :], G["ones"])], ones.b, writes=[ones.b])
        masks = p.tile(es, "masks", [128, 2, 128], BF16)
        p.dma("pool", [(masks.t[:, :, :], G["masks"])], masks.b, writes=[masks.b])
        qT = [p.tile(es, "qT", [128, T], BF16) for _ in range(3)]
        kT = [p.tile(es, "kT", [128, T], BF16) for _ in range(3)]
        vv = [p.tile(es, "vv", [128, 32, 128], BF16) for _ in range(3)]
        num = [p.tile(es, "num", [128, T], F32) for _ in range(3)]
        zacc = p.tile(es, "zacc", [128, T], F32)
        ob = [p.tile(es, "ob", [128, T], BF16) for _ in range(2)]
        ex = [p.tile(es, "ex", [128, 2, 128], BF16) for _ in range(2)]
        cnt = 0
        for h in range(4):
            for g in range(3):
                d = DILS[g]
                nb = T // (128 * d)
                r0 = g * 512 + h * 128
                p.dma("sp", [(qT[g].t[:, :], G["QKT"][r0:r0 + 128, :])], qT[g].b, reads=[G["QKT_b"]], writes=[qT[g].b])
                p.dma("sp", [(kT[g].t[:, :], G["QKT"][1536 + r0:1536 + r0 + 128, :])], kT[g].b, reads=[G["QKT_b"]], writes=[kT[g].b])
                Vv = G["V"][:, r0:r0 + 128].rearrange("(b i r) c -> i r b c", i=128, r=d)
                vt = vv[g].t[:, :, :].rearrange("i (r b) c -> i r b c", r=d)
                parts = max(1, 4 // d)
                bs = nb // parts
                pairs = []
                for r in range(d):
                    for pb in range(parts):
                        pairs.append((vt[:, r, pb * bs:(pb + 1) * bs, :], Vv[:, r, pb * bs:(pb + 1) * bs, :]))
                p.dma("sp", pairs, vv[g].b, reads=[G["V_b"]], writes=[vv[g].b])
            for g in range(3):
                d = DILS[g]
                nb = T // (128 * d)
                qv = qT[g].t[:, :].rearrange("q (b i r) -> q r b i", i=128, r=d)
                kv = kT[g].t[:, :].rearrange("q (b i r) -> q r b i", i=128, r=d)
                nv = num[g].t[:, :].rearrange("q (b i r) -> q r b i", i=128, r=d)
                zv = zacc.t[:, :].rearrange("q (b i r) -> q r b i", i=128, r=d)
                vt = vv[g].t[:, :, :].rearrange("i (r b) c -> i r b c", r=d)
                for r in range(d):
                    for ub in range(nb):
                        ps, pn, pz, ex_ = psS[cnt % 2], psN[cnt % 2], psZ[cnt % 2], ex[cnt % 2]
                        cnt += 1
                        blks = ([(0, ub - 1)] if ub > 0 else []) + [(1, ub)]
                        fns = [(lambda e, m=m, kb=kb, ps=ps, kv=kv, qv=qv, r=r, ub=ub: e.matmul(
                            ps.t[:, m * 128:(m + 1) * 128], lhsT=kv[:, r, kb, :], rhs=qv[:, r, ub, :], start=True, stop=True))
                            for (m, kb) in blks]
                        p.compute("pe", fns, reads=[kT[g].b, qT[g].b], writes=[ps.b])
                        lo = blks[0][0]
                        p.compute("act", lambda e, ps=ps, ex_=ex_, lo=lo: e.activation(
                            out=ex_.t[:, lo:2, :], in_=ps.t[:, lo * 128:256].rearrange("q (m t) -> q m t", t=128), func=AF.Exp, scale=ISQ),
                            reads=[ps.b], writes=[ex_.b])
                        p.compute("dve", lambda e, ex_=ex_, lo=lo: e.tensor_tensor(out=ex_.t[:, lo:2, :], in0=ex_.t[:, lo:2, :],
                                                                                 in1=masks.t[:, lo:2, :], op=ALU.mult),
                                  reads=[ex_.b, masks.b], writes=[ex_.b])
                        nblk = len(blks)
                        fns = [(lambda e, i=i, m=m, kb=kb, pn=pn, vt=vt, ex_=ex_, r=r, nblk=nblk: e.matmul(
                            pn.t[:, :], lhsT=vt[:, r, kb, :], rhs=ex_.t[:, m, :], start=(i == 0), stop=(i == nblk - 1)))
                            for i, (m, kb) in enumerate(blks)]
                        fns += [(lambda e, i=i, m=m, pz=pz, ex_=ex_, nblk=nblk: e.matmul(
                            pz.t[:, :], lhsT=ones.t[:, :], rhs=ex_.t[:, m, :], start=(i == 0), stop=(i == nblk - 1)))
                            for i, (m, kb) in enumerate(blks)]
                        p.compute("pe", fns, reads=[vv[g].b, ex_.b, ones.b], writes=[pn.b, pz.b])
                        p.compute("act", lambda e, pn=pn, nv=nv, r=r, ub=ub: e.copy(out=nv[:, r, ub, :], in_=pn.t[:, :]),
                                  reads=[pn.b], writes=[num[g].b])
                        if g == 0:
                            p.compute("dve", lambda e, pz=pz, zv=zv, r=r, ub=ub: e.tensor_copy(out=zv[:, r, ub, :], in_=pz.t[:, :]),
                                      reads=[pz.b], writes=[zacc.b])
                        else:
                            p.compute("dve", lambda e, pz=pz, zv=zv, r=r, ub=ub: e.tensor_tensor(
                                out=zv[:, r, ub, :], in0=zv[:, r, ub, :], in1=pz.t[:, :], op=ALU.add), reads=[pz.b, zacc.b], writes=[zacc.b])
            p.compute("dve", lambda e: e.reciprocal(out=zacc.t[:, :], in_=zacc.t[:, :]), reads=[zacc.b], writes=[zacc.b])
            for g in range(3):
                o_ = ob[(h * 3 + g) % 2]
                eng = "dve" if g != 1 else "pool"
                p.compute(eng, lambda e, o_=o_, g=g: e.tensor_tensor(out=o_.t[:, :], in0=num[g].t[:, :], in1=zacc.t[:, :], op=ALU.mult),
                          reads=[num[g].b, zacc.b], writes=[o_.b])
                r0 = g * 512 + h * 128
                p.dma("sp", [(G["MIXT"][r0:r0 + 128, :], o_.t[:, :])], o_.b, reads=[o_.b], writes=[G["MIXT_b"]])
        p.flush()


def phase_Q(p, G, l, cur):
    NEGB = -1.0e30
    with ExitStack() as es:
        psr = PsumRing([p.psum(es, "ps", [128, 512], F32) for _ in range(8)])
        wq = p.tile(es, "wq", [128, 16, 2048], BF16)
        load_w_cast(p, wq, G["peer_wq"][l], 16, sub=8)
        keysT = p.tile(es, "keysT", [128, 2048], BF16)
        p.dma("pool", [(keysT.t[:, :], G["keysT"][l])], keysT.b, writes=[keysT.b])
        cin = [p.tile(es, "cin", [128, 2048], F32) for _ in range(2)]
        cout = [p.tile(es, "cout", [128, 2048], BF16) for _ in range(2)]
        cast_units = []
        if "UB" in G:
            uT = G["peer_uT"][l].rearrange("(k q) e -> q k e", q=128)
            for g in range(NG):
                for hf in range(4):
                    cast_units.append((uT[:, hf * 4:(hf + 1) * 4, g * EG:(g + 1) * EG], G["UB"][g][:, hf * 4:(hf + 1) * 4, :],
                                       "q (k e) -> q k e", 4, G["UB_b"]))
                    cast_units.append((G["peer_v"][l][g * EG + hf * 128:g * EG + (hf + 1) * 128, :].rearrange("(c q) d -> q c d", q=128),
                                       G["VB"][g][:, hf:hf + 1, :], "q (k e) -> q k e", 1, G["VB_b"]))
        cast_state = {"i": 0}

        def emit_cast(n):
            for _ in range(n):
                i = cast_state["i"]
                if i >= len(cast_units):
                    return
                cast_state["i"] = i + 1
                src, dst, pat, kk, db = cast_units[i]
                ci, co = cin[i % 2], cout[i % 2]
                p.dma("sp", [(ci.t[:, :].rearrange(pat, k=kk), src)], ci.b, writes=[ci.b])
                if i % 2 == 0:
                    p.compute("pool", lambda e, ci=ci, co=co: e.tensor_copy(out=co.t[:, :], in_=ci.t[:, :]), reads=[ci.b], writes=[co.b])
                else:
                    p.compute("act", lambda e, ci=ci, co=co: e.copy(out=co.t[:, :], in_=ci.t[:, :]), reads=[ci.b], writes=[co.b])
                p.dma("sp", [(dst, co.t[:, :].rearrange(pat, k=kk))], co.b, reads=[co.b], writes=[db])

        xTb = [p.tile(es, "xTb", [128, 16, 512], BF16) for _ in range(2)]
        qT = p.tile(es, "qT", [128, 16, 512], BF16)
        Sx = [p.tile(es, "Sx", [128, 2048], F32) for _ in range(2)]
        work = p.tile(es, "work", [128, 16, 128], F32)
        fz = p.tile(es, "fz", [128, 1], F32)
        top = p.tile(es, "top", [128, 8, 2, 16], F32)
        cand = p.tile(es, "cand", [128, 8, 256], F32)
        cwork = p.tile(es, "cwork", [128, 256], F32)
        ctop = p.tile(es, "ctop", [128, 8, 16], F32)
        cex = p.tile(es, "cex", [128, 8, 16], F32)
        zs = p.tile(es, "zs", [128, 8], F32)
        tn = [p.tile(es, "tn", [128, 16], F32) for _ in range(2)]
        XTv = xT_view(G["XT"][cur])
        def load_xTb(bj):
            x_ = xTb[bj % 2]
            p.dma("sp", [(x_.t[:, :, :], XTv[:, :, bj * 512:(bj + 1) * 512])], x_.b, reads=[G["XT_b"][cur]], writes=[x_.b])

        load_xTb(0)
        for bi in range(NBLK):
            xT = xTb[bi % 2]
            if bi + 1 < NBLK:
                load_xTb(bi + 1)
            for j in range(16):
                ps = psr.next()
                fns = [(lambda e, k=k, j=j, ps=ps, xT=xT: e.matmul(ps.t[:, :], lhsT=wq.t[:, k, j * 128:(j + 1) * 128],
                                                                   rhs=xT.t[:, k, :], start=(k == 0), stop=(k == 15)))
                       for k in range(16)]
                p.compute("pe", fns, reads=[wq.b, xT.b], writes=[ps.b])
                p.compute("act", lambda e, j=j, ps=ps: e.copy(out=qT.t[:, j, :], in_=ps.t[:, :]), reads=[ps.b], writes=[qT.b])
            for tt in range(4):
                ti = bi * 4 + tt
                S_ = Sx[ti % 2]
                emit_cast(8)
                for bk in range(4):
                    ps = psr.next()
                    fns = [(lambda e, j=j, ps=ps, tt=tt: e.matmul(ps.t[:, (j % 4) * 128:(j % 4 + 1) * 128],
                                                                  lhsT=qT.t[:, j, tt * 128:(tt + 1) * 128],
                                                                  rhs=keysT.t[:, j * 128:(j + 1) * 128], start=True, stop=True))
                           for j in range(bk * 4, bk * 4 + 4)]
                    p.compute("pe", fns, reads=[qT.b, keysT.b], writes=[ps.b])
                    p.compute("act", lambda e, bk=bk, ps=ps, S_=S_: e.copy(out=S_.t[:, bk * 512:(bk + 1) * 512], in_=ps.t[:, :]),
                              reads=[ps.b], writes=[S_.b])
                p.dma("sp", [(G["SALL"][ti * 128:(ti + 1) * 128, :], S_.t[:, :])], S_.b, reads=[S_.b], writes=[G["SALL_b"]])
                tb = [Buf("tb%d" % j) for j in range(16)]
                wb = [Buf("wb%d" % j) for j in range(16)]
                p.compute("dve", lambda e: e.memset(fz.t[:, :], 0.0), writes=[top.b, work.b, fz.b])
                for j in range(16):
                    h, pp = j // 2, j % 2
                    sj = S_.t[:, j * 128:(j + 1) * 128]
                    p.compute("dve", lambda e, sj=sj, h=h, pp=pp: e.max(out=top.t[:, h, pp, 0:8], in_=sj),
                              reads=[S_.b, top.b], writes=[tb[j]])
                for j in range(16):
                    h, pp = j // 2, j % 2
                    sj = S_.t[:, j * 128:(j + 1) * 128]
                    p.compute("dve", lambda e, sj=sj, h=h, pp=pp, j=j: e.match_replace(
                        out=work.t[:, j, :], in_to_replace=top.t[:, h, pp, 0:8], in_values=sj, imm_value=NEGB),
                        reads=[S_.b, tb[j], work.b, top.b], writes=[wb[j]])
                for j in range(16):
                    h, pp = j // 2, j % 2
                    p.compute("dve", lambda e, h=h, pp=pp, j=j: e.max(out=top.t[:, h, pp, 8:16], in_=work.t[:, j, :]),
                              reads=[wb[j], top.b, work.b], writes=[tb[j]])
                p.compute("dve", lambda e: e.memset(fz.t[:, :], 0.0), reads=tb, writes=[top.b, work.b, fz.b])
                p.compute("dve", lambda e: e.tensor_tensor(
                    out=cand.t[:, :, :].rearrange("q h (a b) -> q h a b", a=16),
                    in0=top.t[:, :, 0, :].unsqueeze(3).to_broadcast([128, 8, 16, 16]),
                    in1=top.t[:, :, 1, :].unsqueeze(2).to_broadcast([128, 8, 16, 16]), op=ALU.add),
                    reads=[top.b], writes=[cand.b])
                fns = []
                for h in range(8):
                    fns.append(lambda e, h=h: e.max(out=ctop.t[:, h, 0:8], in_=cand.t[:, h, :]))
                    fns.append(lambda e, h=h: e.match_replace(out=cwork.t[:, :], in_to_replace=ctop.t[:, h, 0:8],
                                                              in_values=cand.t[:, h, :], imm_value=NEGB))
                    fns.append(lambda e, h=h: e.max(out=ctop.t[:, h, 8:16], in_=cwork.t[:, :]))
                for f in fns:
                    p.compute("dve", f, reads=[cand.b, ctop.b, cwork.b], writes=[ctop.b, cwork.b])
                tn_ = tn[ti % 2]
                p.compute("dve", lambda e: e.tensor_tensor(out=cex.t[:, :, :], in0=ctop.t[:, :, :],
                                                           in1=ctop.t[:, :, 0:1].to_broadcast([128, 8, 16]), op=ALU.subtract),
                          reads=[ctop.b], writes=[cex.b])
                p.compute("act", lambda e: e.activation(out=cex.t[:, :, :], in_=cex.t[:, :, :], func=AF.Exp), reads=[cex.b], writes=[cex.b])
                p.compute("dve", lambda e: e.reduce_sum(out=zs.t[:, :], in_=cex.t[:, :, :], axis=mybir.AxisListType.X),
                          reads=[cex.b], writes=[zs.b])
                p.compute("act", lambda e: e.activation(out=zs.t[:, :], in_=zs.t[:, :], func=AF.Ln), reads=[zs.b], writes=[zs.b])
                p.compute("dve", lambda e, tn_=tn_: e.scalar_tensor_tensor(
                    out=tn_.t[:, 8:16], in0=ctop.t[:, :, 0], scalar=-1.0, in1=zs.t[:, :], op0=ALU.mult, op1=ALU.subtract),
                    reads=[ctop.b, zs.b], writes=[tn_.b])
                p.compute("dve", lambda e, tn_=tn_: e.tensor_copy(out=tn_.t[:, 0:8], in_=ctop.t[:, :, 15]), reads=[ctop.b], writes=[tn_.b])
                p.dma("sp", [(G["TN"][ti * 128:(ti + 1) * 128, :], tn_.t[:, :])], tn_.b, reads=[tn_.b], writes=[G["TN_b"]])
        p.flush()


def phase_PEER(p, G, l, cur):
    NGT = NBLK * NG
    with ExitStack() as es:
        psA = [p.psum(es, "psA", [128, 512], F32) for _ in range(2)]
        psG = [p.psum(es, "psG", [128, 512], F32) for _ in range(2)]
        psY = p.psum(es, "psY", [128, 2048], F32)
        ident = p.tile(es, "ident", [128, 128], BF16)
        p.dma("pool", [(ident.t[:, :], G["ident"])], ident.b, writes=[ident.b])
        xT = p.tile(es, "xT", [128, 16, 512], BF16)
        Sx = [p.tile(es, "Sx", [128, 8, 2, 128], F32) for _ in range(4)]
        tn = [p.tile(es, "tn", [128, 16], F32) for _ in range(4)]
        thn = [p.tile(es, "thn", [128, 8], F32) for _ in range(4)]
        yacc = [p.tile(es, "yacc", [128, 2048], F32) for _ in range(4)]
        ug = [p.tile(es, "ug", [128, 16, EG], BF16) for _ in range(2)]
        vg = [p.tile(es, "vg", [128, 4, 2048], BF16) for _ in range(2)]
        zz = [p.tile(es, "zz", [128, 8, 4, 128], F32) for _ in range(2)]
        zzB = [Buf("zzB0"), Buf("zzB1")]
        ee = [p.tile(es, "ee", [128, 8, 4, 128], BF16) for _ in range(2)]
        aS = [p.tile(es, "aS", [128, 4, 512], BF16) for _ in range(2)]
        WT = [p.tile(es, "WT", [128, 4, 128], BF16) for _ in range(2)]
        XTv = xT_view(G["XT"][cur])
        HP = 3

        def load_u(gi):
            u_ = ug[gi % 2]
            p.dma("sp", [(u_.t[:, :, :], G["UB"][gi % NG])], u_.b, reads=[G["UB_b"]], writes=[u_.b])

        def load_v(gi):
            v_ = vg[gi % 2]
            p.dma("sp", [(v_.t[:, :, :], G["VB"][gi % NG])], v_.b, reads=[G["VB_b"]], writes=[v_.b])

        def load_xT(bi):
            p.dma("sp", [(xT.t[:, :, :], XTv[:, :, bi * 512:(bi + 1) * 512])], xT.b, reads=[G["XT_b"][cur]], writes=[xT.b])

        acnt = [0]

        def emit_aT(gi, c):
            pa = psA[acnt[0] % 2]
            acnt[0] += 1
            u_, a_ = ug[gi % 2], aS[gi % 2]
            fns = [(lambda e, k=k, c=c, pa=pa, u_=u_: e.matmul(pa.t[:, :], lhsT=u_.t[:, k, c * 128:(c + 1) * 128],
                                                              rhs=xT.t[:, k, :], start=(k == 0), stop=(k == 15)))
                   for k in range(16)]
            p.compute("pe", fns, reads=[xT.b, u_.b], writes=[pa.b])
            p.compute("act", lambda e, pa=pa, a_=a_, c=c: e.copy(out=a_.t[:, c, :], in_=pa.t[:, :]), reads=[pa.b], writes=[a_.b])

        def front(g, tt, n):
            pb = n % 2
            z_, e_, S_ = zz[pb], ee[pb], Sx[tt]
            p.compute("pool", lambda e, S_=S_, g=g, z_=z_: e.tensor_tensor(
                out=z_.t[:, 0:HP, :, :],
                in0=S_.t[:, 0:HP, 0, g * 4:(g + 1) * 4].unsqueeze(3).to_broadcast([128, HP, 4, 128]),
                in1=S_.t[:, 0:HP, 1, :].unsqueeze(2).to_broadcast([128, HP, 4, 128]), op=ALU.add),
                reads=[S_.b], writes=[z_.b])
            p.compute("dve", lambda e, S_=S_, g=g, z_=z_: e.tensor_tensor(
                out=z_.t[:, HP:8, :, :],
                in0=S_.t[:, HP:8, 0, g * 4:(g + 1) * 4].unsqueeze(3).to_broadcast([128, 8 - HP, 4, 128]),
                in1=S_.t[:, HP:8, 1, :].unsqueeze(2).to_broadcast([128, 8 - HP, 4, 128]), op=ALU.add),
                reads=[S_.b], writes=[zzB[pb]])
            p.compute("act", lambda e, z_=z_, e_=e_: e.activation(out=e_.t[:, :, :, :], in_=z_.t[:, :, :, :], func=AF.Exp),
                      reads=[z_.b, zzB[pb]], writes=[e_.b])

        def stageB(gi, tt, n):
            pb = n % 2
            pg, z_, e_ = psG[pb], zz[pb], ee[pb]
            th_ = thn[tt]
            p.compute("dve", [(lambda e, h=h, th_=th_, z_=z_, e_=e_: e.scalar_tensor_tensor(
                out=e_.t[:, h, :, :], in0=z_.t[:, h, :, :], scalar=th_.t[:, h:h + 1], in1=e_.t[:, h, :, :],
                op0=ALU.is_ge, op1=ALU.mult)) for h in range(8)], reads=[z_.b, zzB[pb], e_.b, th_.b], writes=[e_.b])
            fns = [(lambda e, c=c, h=h, pg=pg, e_=e_: e.matmul(pg.t[:, c * 128:(c + 1) * 128], lhsT=e_.t[:, h, c, :], rhs=ident.t[:, :],
                                                              start=(h == 0), stop=(h == 7))) for c in range(4) for h in range(8)]
            p.compute("pe", fns, reads=[e_.b, ident.b], writes=[pg.b])

        def stageC(gi, tt, n):
            pb = n % 2
            v_, a_ = vg[gi % 2], aS[gi % 2]
            pg, W_ = psG[pb], WT[pb]
            p.compute("dve", lambda e, pg=pg, a_=a_, W_=W_, tt=tt: e.tensor_tensor(
                out=W_.t[:, :, :], in0=pg.t[:, :].rearrange("q (c t) -> q c t", c=4), in1=a_.t[:, :, tt * 128:(tt + 1) * 128],
                op=ALU.mult), reads=[pg.b, a_.b], writes=[W_.b])
            fns = [(lambda e, c=c, db=db, v_=v_, W_=W_: e.matmul(psY.t[:, db * 512:(db + 1) * 512], lhsT=W_.t[:, c, :],
                                                                rhs=v_.t[:, c, db * 512:(db + 1) * 512], start=(c == 0), stop=(c == 3)))
                   for db in range(4) for c in range(4)]
            p.compute("pe", fns, reads=[W_.b, v_.b], writes=[psY.b])

        def stageD(gi, tt, n):
            g = gi % NG
            ya = yacc[tt]
            if g == 0:
                p.compute("dve", lambda e, ya=ya: e.tensor_copy(out=ya.t[:, :], in_=psY.t[:, :]), reads=[psY.b], writes=[ya.b])
            else:
                p.compute("dve", lambda e, ya=ya: e.tensor_tensor(out=ya.t[:, :], in0=ya.t[:, :], in1=psY.t[:, :], op=ALU.add),
                          reads=[psY.b, ya.b], writes=[ya.b])

        load_u(0)
        load_u(1)
        load_v(0)
        load_xT(0)
        for tt in range(4):
            emit_aT(0, tt)
        n = 0
        for bi in range(NBLK):
            for tt in range(4):
                ti = bi * 4 + tt
                S_, tn_ = Sx[tt], tn[tt]
                p.dma("sp", [(S_.t[:, :, :, :], G["SALL"][ti * 128:(ti + 1) * 128, :].rearrange("q (h s n) -> q h s n", h=8, s=2))],
                      S_.b, reads=[G["SALL_b"]], writes=[S_.b])
                p.dma("sp", [(tn_.t[:, :], G["TN"][ti * 128:(ti + 1) * 128, :])], tn_.b, reads=[G["TN_b"]], writes=[tn_.b])
                p.compute("dve", lambda e, S_=S_, tn_=tn_: e.tensor_tensor(
                    out=S_.t[:, :, 0, :], in0=S_.t[:, :, 0, :], in1=tn_.t[:, 8:16].unsqueeze(2).to_broadcast([128, 8, 128]), op=ALU.add),
                    reads=[S_.b, tn_.b], writes=[S_.b])
                p.compute("dve", lambda e, tn_=tn_, th_=thn[tt]: e.tensor_tensor(out=th_.t[:, :], in0=tn_.t[:, 0:8], in1=tn_.t[:, 8:16], op=ALU.add),
                          reads=[tn_.b], writes=[thn[tt].b])
            units = [(g, tt) for g in range(NG) for tt in range(4)]
            L = len(units)
            base = bi * L
            for i in range(-2, L + 1):
                if 0 <= i - 1 < L:
                    g, tt = units[i - 1]
                    stageD(bi * NG + g, tt, base + i - 1)
                if 0 <= i < L:
                    g, tt = units[i]
                    gi = bi * NG + g
                    if tt == 0:
                        if gi + 2 < NGT:
                            load_u(gi + 2)
                        if gi + 1 < NGT:
                            load_v(gi + 1)
                        if g == NG - 1 and bi + 1 < NBLK:
                            load_xT(bi + 1)
                        a_ = aS[gi % 2]
                        p.compute("act", lambda e, a_=a_: e.activation(out=a_.t[:, :, :], in_=a_.t[:, :, :], func=AF.Gelu),
                                  reads=[a_.b], writes=[a_.b])
                    stageC(gi, tt, base + i)
                    if gi + 1 < NGT:
                        emit_aT(gi + 1, tt)
                if 0 <= i + 1 < L:
                    g, tt = units[i + 1]
                    stageB(bi * NG + g, tt, base + i + 1)
                if 0 <= i + 2 < L:
                    g, tt = units[i + 2]
                    front(g, tt, base + i + 2)
            for tt in range(4):
                ti = bi * 4 + tt
                p.dma("sp", [(G["Y"][ti * 128:(ti + 1) * 128, :], yacc[tt].t[:, :])], yacc[tt].b, reads=[yacc[tt].b], writes=[G["Y_b"]])
        p.flush()


def build_program(steps=None, with_peer=True):
    nc = bass.Bass("TRN2", target_bir_lowering=False)
    G = {}

    def din(name, shape):
        G[name] = nc.dram_tensor(name, list(shape), F32, kind="ExternalInput").ap()

    din("x", [T, D])
    din("mem", [256, D])
    din("w_in_a", [2, D, 2048])
    din("w_pool", [2, 4, 384, 384])
    din("s_pool_l", [2, 128, 12])
    din("w_in_b", [2, D, 5120])
    din("w_mem_kv", [4, D, 1024])
    din("w_o", [4, D, D])
    din("ln_g", [4, 2, D])
    din("ln_b", [4, 2, D])
    din("peer_wq", [4, D, D])
    din("keysT", [4, 128, 2048])
    if with_peer:
        din("peer_uT", [4, D, NEXP])
        din("peer_v", [4, NEXP, D])
    din("ident", [128, 128])
    din("ones", [128, 128])
    din("rc", [128, 4, 512])
    din("masks", [128, 2, 128])

    def dscr(name, shape, dt):
        G[name] = nc.dram_tensor(name, list(shape), dt, kind="Internal").ap()
        G[name + "_b"] = Buf(name, accum=True)

    G["X"], G["X_b"], G["XT"], G["XT_b"] = [], [], [], []
    for i in range(2):
        G["X"].append(nc.dram_tensor("X%d" % i, [T, D], F32, kind="Internal").ap())
        G["X_b"].append(Buf("X%d" % i, accum=True))
        G["XT"].append(nc.dram_tensor("XT%d" % i, [D, T], BF16, kind="Internal").ap())
        G["XT_b"].append(Buf("XT%d" % i, accum=True))
    dscr("MEMT", [D, 256], BF16)
    dscr("MIXT", [D, T], BF16)
    out = nc.dram_tensor("out", [T, D], F32, kind="ExternalOutput").ap()
    G["out"] = out

    dscr("SALL", [T, 2048], F32)
    dscr("QKT", [3072, T], BF16)
    dscr("V", [T, 1536], BF16)
    dscr("TN", [T, 16], F32)
    dscr("Y", [T, D], F32)
    if with_peer:
        dscr("UB", [NG, 128, 16, EG], BF16)
        dscr("VB", [NG, 128, 4, D], BF16)
    if steps is None:
        steps = [("init",)]
        for l in range(DEPTH):
            if l % 2 == 0:
                steps += [("A1", l, l // 2)]
            else:
                steps += [("B1", l, l // 2), ("B2", l)]
            steps += [("O", l), ("Q", l), ("PEER", l), ("F", l)]
        steps += [("out",)]

    with ExitStack() as es:
        p = Prog(nc, es)
        cur = 0
        for st in steps:
            k = st[0]
            if k == "init":
                phase_init(p, G)
                G["X"][0] = G["x"]
            elif k == "A1":
                phase_A1(p, G, st[1], st[2], cur)
            elif k == "B1":
                phase_B1(p, G, st[1], st[2], cur)
            elif k == "B2":
                phase_B2(p, G, st[1], cur)
            elif k == "O":
                phase_O(p, G, st[1], cur, 0)
                cur = 1 - cur
            elif k == "Q":
                phase_Q(p, G, st[1], cur)
            elif k == "PEER":
                phase_PEER(p, G, st[1], cur)
            elif k == "F":
                phase_O(p, G, st[1], cur, 1)
                cur = 1 - cur
            elif k == "out":
                with ExitStack() as es2:
                    tl = [p.tile(es2, "cp", [128, D], F32) for _ in range(4)]
                    for ti in range(NT):
                        t_ = tl[ti % 4]
                        p.dma("sp", [(t_.t[:, :], G["X"][cur][ti * 128:(ti + 1) * 128, :])], t_.b, reads=[G["X_b"][cur]], writes=[t_.b])
                        p.dma("sp", [(out[ti * 128:(ti + 1) * 128, :], t_.t[:, :])], t_.b, reads=[t_.b])
                    p.flush()
            elif k == "dump":
                name, shape, dt = st[1], st[2], st[3]
                src = G[name] if not isinstance(G[name], list) else G[name][st[4]]
                srcb = G[name + "_b"] if not isinstance(G[name + "_b"], list) else G[name + "_b"][st[4]]
                dbg = nc.dram_tensor("dbg", list(shape), dt, kind="ExternalOutput").ap()
                with ExitStack() as es2:
                    tl = [p.tile(es2, "cpd", [128, shape[1]], dt) for _ in range(2)]
                    for c in range(shape[0] // 128):
                        t_ = tl[c % 2]
                        p.dma("sp", [(t_.t[:, :], src[c * 128:(c + 1) * 128, :])], t_.b, reads=[srcb], writes=[t_.b])
                        p.dma("sp", [(dbg[c * 128:(c + 1) * 128, :], t_.t[:, :])], t_.b, reads=[t_.b])
                    p.flush()
    return nc


def host_consts():
    ident = np.eye(128, dtype=np.float32)
    ones = np.ones((128, 128), dtype=np.float32)
    rc = np.zeros((128, 4, 512), dtype=np.float32)
    t = np.arange(512)
    for g in range(4):
        w = 2 << g
        rc[:, g, :] = 1.0 / np.minimum(t + 1, w).astype(np.float32)
    k = np.arange(128)[:, None]
    q = np.arange(128)[None, :]
    masks = np.stack([(k >= q), (k <= q)], axis=1).astype(np.float32)
    return {"ident": ident, "ones": ones, "rc": rc, "masks": masks}


def host_layout(inputs):
    w = {}
    for k in ("w_in_a", "w_pool", "w_in_b", "w_mem_kv", "w_o", "ln_g", "ln_b", "peer_wq", "peer_v"):
        w[k] = np.ascontiguousarray(np.asarray(inputs[k], dtype=np.float32))
    sp = np.asarray(inputs["s_pool"], dtype=np.float32)
    w["s_pool_l"] = np.ascontiguousarray(sp.reshape(2, 12, 128).transpose(0, 2, 1))
    pk = np.asarray(inputs["peer_keys"], dtype=np.float32)
    w["keysT"] = np.ascontiguousarray(pk.transpose(0, 4, 1, 2, 3).reshape(4, 128, 2048))
    pu = np.asarray(inputs["peer_u"], dtype=np.float32)
    w["peer_uT"] = np.ascontiguousarray(pu.transpose(0, 2, 1))
    w.update(host_consts())
    return w


def kernel(**inputs):
    x = np.asarray(inputs["x"], dtype=np.float32)
    mem = np.asarray(inputs["mem"], dtype=np.float32)
    w = host_layout(inputs)
    nc = build_program()
    ncores = 8
    in_maps = []
    for c in range(ncores):
        m = dict(w)
        m["x"] = np.ascontiguousarray(x[c])
        m["mem"] = np.ascontiguousarray(mem[c])
        in_maps.append(m)
    res = run_bass_kernel_spmd(nc, in_maps, core_ids=list(range(ncores)))
    return np.stack([np.asarray(r["out"], dtype=np.float32) for r in res.results], axis=0)
```

```python
import math
from contextlib import ExitStack

import numpy as np

import concourse.bass as bass
import concourse.mybir as mybir
from concourse.bass_utils import run_bass_kernel_spmd

F32 = mybir.dt.float32
BF16 = mybir.dt.bfloat16
AF = mybir.ActivationFunctionType
ALU = mybir.AluOpType

T = 4096
D = 2048
NT = T // 128
NBLK = T // 512
DEPTH = 4
LOCAL = 1536
ALPHA = (2 * DEPTH) ** 0.25
EPS = 1e-5
ISQ = 128 ** -0.5
NEXP = 16384
EG = 512
SAME_ENGINE_SYNC = True
NG = NEXP // EG


class Sem:
    __slots__ = ("h", "count")

    def __init__(self, h):
        self.h = h
        self.count = 0


class Buf:
    __slots__ = ("w", "r", "accum", "dsem", "name")

    def __init__(self, name="", accum=False):
        self.w = {}
        self.r = {}
        self.accum = accum
        self.dsem = None
        self.name = name


class Tile:
    __slots__ = ("t", "b")

    def __init__(self, t, b):
        self.t = t
        self.b = b


class Prog:
    ENGS = ("sp", "pe", "act", "dve", "pool")
    ATTR = {"sp": "sync", "pe": "tensor", "act": "scalar", "dve": "vector", "pool": "gpsimd"}

    def __init__(self, nc, es, n_dma_sems=40):
        self.nc = nc
        self.esem = {e: Sem(es.enter_context(nc.semaphore("se_" + e))) for e in ("pe", "act", "dve", "pool")}
        self.dpool = [Sem(es.enter_context(nc.semaphore("sd%d" % i))) for i in range(n_dma_sems)]
        self.dnext = 0
        self.ops = {e: [] for e in self.ENGS}
        self.seen = {e: {} for e in self.ENGS}
        self.nps = 0
        self.uid = 0

    def tile(self, es, name, shape, dt):
        self.uid += 1
        t = es.enter_context(self.nc.sbuf_tensor("%s_%d" % (name, self.uid), list(shape), dt))
        return Tile(t, Buf(name))

    def psum(self, es, name, shape, dt):
        self.uid += 1
        t = es.enter_context(self.nc.psum_tensor("%s_%d" % (name, self.uid), list(shape), dt))
        return Tile(t, Buf(name))

    def _dsem(self, buf):
        if buf.dsem is None:
            assert self.dnext < len(self.dpool), "out of DMA semaphores"
            buf.dsem = self.dpool[self.dnext]
            self.dnext += 1
        return buf.dsem

    def _record(self, eng, fns, reads, writes, sem, inc_each):
        deps = {}

        def merge(d):
            for s, v in d.items():
                if deps.get(s, 0) < v:
                    deps[s] = v

        for b in reads:
            merge(b.w)
        for b in writes:
            merge(b.r)
            if not b.accum:
                merge(b.w)
        n = len(fns)
        if inc_each:
            sem.count += 16 * n
        else:
            sem.count += 1
        val = sem.count
        seen = self.seen[eng]
        waits = []
        own = self.esem.get(eng)
        for s, v in deps.items():
            if s is own and (eng == "pe" or not SAME_ENGINE_SYNC):
                continue
            if seen.get(s, 0) >= v:
                continue
            seen[s] = v
            waits.append((s, v))
        self.ops[eng].append((waits, fns, sem, inc_each))
        for b in reads:
            if b.r.get(sem, 0) < val:
                b.r[sem] = val
        for b in writes:
            if b.accum:
                if b.w.get(sem, 0) < val:
                    b.w[sem] = val
            else:
                b.w = {sem: val}
                b.r = {}

    def compute(self, eng, fns, reads=(), writes=()):
        if not isinstance(fns, (list, tuple)):
            fns = [fns]
        self._record(eng, list(fns), reads, writes, self.esem[eng], False)

    def dma(self, eng, pairs, sbuf, reads=(), writes=()):
        sem = self._dsem(sbuf)
        fns = [(lambda e, o=o, i=i: e.dma_start(out=o, in_=i)) for (o, i) in pairs]
        self._record(eng, fns, reads, writes, sem, True)

    def flush(self):
        waits = []
        seen = self.seen["sp"]
        for s in self.dpool[: self.dnext]:
            if s.count > 0 and seen.get(s, 0) < s.count:
                seen[s] = s.count
                waits.append((s, s.count))
        self.ops["sp"].append((waits, [], None, False))
        nc = self.nc
        with nc.Block() as block:
            for eng in self.ENGS:
                ops = self.ops[eng]

                def body(e, ops=ops):
                    for waits, fns, sem, inc_each in ops:
                        for s, v in waits:
                            e.wait_ge(s.h, v)
                        n = len(fns)
                        for i, f in enumerate(fns):
                            ins = f(e)
                            if inc_each:
                                ins.then_inc(sem.h, 16)
                            elif i == n - 1:
                                ins.then_inc(sem.h, 1)

                getattr(block, self.ATTR[eng])(body)
        self.nps += sum(len(o[1]) for e in self.ENGS for o in self.ops[e])
        self.ops = {e: [] for e in self.ENGS}
        self.dnext = 0


class PsumRing:
    def __init__(self, tiles):
        self.tiles = tiles
        self.i = 0

    def next(self):
        t = self.tiles[self.i % len(self.tiles)]
        self.i += 1
        return t


def load_w_cast(p, dst, dram_rows_ap, kchunks, sub=4):
    v = dram_rows_ap.rearrange("(k q) c -> q k c", q=128)
    pairs = []
    step = max(1, kchunks // sub)
    for k0 in range(0, kchunks, step):
        pairs.append((dst.t[:, k0:k0 + step, :], v[:, k0:k0 + step, :]))
    p.dma("pool", pairs, dst.b, writes=[dst.b])


def emit_ln_tile(p, S, r, xn, lng, lnb):
    mv, rstd, eps, junk = S["ln_mv"], S["ln_rstd"], S["eps"], S["junk"]
    p.compute("dve", lambda e: e.reduce_sum(out=mv.t[:, 0:1], in_=r.t[:, :], axis=mybir.AxisListType.X),
              reads=[r.b], writes=[mv.b])
    p.compute("dve", lambda e: e.tensor_scalar(out=mv.t[:, 0:1], in0=mv.t[:, 0:1], scalar1=-1.0 / D, scalar2=None, op0=ALU.mult),
              reads=[mv.b], writes=[mv.b])
    p.compute("act", lambda e: e.activation(out=junk.t[:, :], in_=r.t[:, :], func=AF.Square, bias=mv.t[:, 0:1], scale=1.0,
                                            accum_out=rstd.t[:, 0:1]), reads=[r.b, mv.b], writes=[junk.b, rstd.b])
    p.compute("act", lambda e: e.activation(out=rstd.t[:, 0:1], in_=rstd.t[:, 0:1], func=AF.Ln, bias=eps.t[:, 0:1], scale=1.0 / D),
              reads=[rstd.b, eps.b], writes=[rstd.b])
    p.compute("act", lambda e: e.activation(out=rstd.t[:, 0:1], in_=rstd.t[:, 0:1], func=AF.Exp, scale=-0.5),
              reads=[rstd.b], writes=[rstd.b])
    p.compute("dve", lambda e: e.tensor_tensor(out=mv.t[:, 1:2], in0=mv.t[:, 0:1], in1=rstd.t[:, 0:1], op=ALU.mult),
              reads=[mv.b, rstd.b], writes=[mv.b])
    p.compute("act", lambda e: e.activation(out=xn.t[:, :], in_=r.t[:, :], func=AF.Identity, bias=mv.t[:, 1:2], scale=rstd.t[:, 0:1]),
              reads=[r.b, mv.b, rstd.b], writes=[xn.b])
    p.compute("dve", lambda e: e.tensor_tensor(out=xn.t[:, :], in0=xn.t[:, :], in1=lng.t[:, :], op=ALU.mult),
              reads=[xn.b, lng.b], writes=[xn.b])
    p.compute("dve", lambda e: e.tensor_tensor(out=xn.t[:, :], in0=xn.t[:, :], in1=lnb.t[:, :], op=ALU.add),
              reads=[xn.b, lnb.b], writes=[xn.b])


def emit_transpose_tile(p, S, src, xTblk, tt, ident, pst):
    xb = S["xb"]
    p.compute("act", lambda e: e.copy(out=xb.t[:, :], in_=src.t[:, :]), reads=[src.b], writes=[xb.b])
    for half in range(2):
        ps = pst.next()
        fns = [(lambda e, c=c, ps=ps: e.transpose(ps.t[:, (c % 8) * 128:(c % 8 + 1) * 128],
                                                   xb.t[:, c * 128:(c + 1) * 128], ident.t[:, :]))
               for c in range(half * 8, half * 8 + 8)]
        p.compute("pe", fns, reads=[xb.b, ident.b], writes=[ps.b])
        p.compute("act", lambda e, ps=ps, half=half: e.copy(
            out=xTblk.t[:, half * 8:half * 8 + 8, tt * 128:(tt + 1) * 128],
            in_=ps.t[:, :].rearrange("q (c t) -> q c t", c=8)), reads=[ps.b], writes=[xTblk.b])


def xT_view(XT):
    return XT.rearrange("(c q) t -> q c t", q=128)


def phase_init(p, G):
    nc = p.nc
    with ExitStack() as es:
        ident = p.tile(es, "ident", [128, 128], BF16)
        p.dma("pool", [(ident.t[:, :], G["ident"])], ident.b, writes=[ident.b])
        S = {"xb": p.tile(es, "xb", [128, D], BF16)}
        pst = PsumRing([p.psum(es, "pst", [128, 1024], BF16) for _ in range(4)])
        xin = [p.tile(es, "xin", [128, D], F32) for _ in range(3)]
        xTb = [p.tile(es, "xTb", [128, 16, 512], BF16) for _ in range(2)]
        memT = p.tile(es, "memT", [128, 16, 256], BF16)
        for mt in range(2):
            xt = xin[mt % 3]
            p.dma("sp", [(xt.t[:, :], G["mem"][mt * 128:(mt + 1) * 128, :])], xt.b, writes=[xt.b])
            xb = S["xb"]
            p.compute("act", lambda e, xt=xt: e.copy(out=xb.t[:, :], in_=xt.t[:, :]), reads=[xt.b], writes=[xb.b])
            for half in range(2):
                ps = pst.next()
                fns = [(lambda e, c=c, ps=ps: e.transpose(ps.t[:, (c % 8) * 128:(c % 8 + 1) * 128],
                                                           xb.t[:, c * 128:(c + 1) * 128], ident.t[:, :]))
                       for c in range(half * 8, half * 8 + 8)]
                p.compute("pe", fns, reads=[xb.b, ident.b], writes=[ps.b])
                p.compute("act", lambda e, ps=ps, half=half, mt=mt: e.copy(
                    out=memT.t[:, half * 8:half * 8 + 8, mt * 128:(mt + 1) * 128],
                    in_=ps.t[:, :].rearrange("q (c t) -> q c t", c=8)), reads=[ps.b], writes=[memT.b])
        p.dma("sp", [(xT_view(G["MEMT"]), memT.t[:, :, :])], memT.b, reads=[memT.b], writes=[G["MEMT_b"]])
        for bi in range(NBLK):
            blk = xTb[bi % 2]
            for tt in range(4):
                ti = bi * 4 + tt
                xt = xin[ti % 3]
                p.dma("sp", [(xt.t[:, :], G["x"][ti * 128:(ti + 1) * 128, :])], xt.b, writes=[xt.b])
                emit_transpose_tile(p, S, xt, blk, tt, ident, pst)
            p.dma("sp", [(xT_view(G["XT"][0])[:, :, bi * 512:(bi + 1) * 512], blk.t[:, :, :])], blk.b,
                  reads=[blk.b], writes=[G["XT_b"][0]])
        p.flush()


def emit_mem_kv(p, es, wkv_dram, G):
    kmT = p.tile(es, "kmT", [128, 4, 256], BF16)
    vm = p.tile(es, "vm", [128, 2, 512], BF16)
    with ExitStack() as es2:
        psr = PsumRing([p.psum(es2, "psk", [128, 512], F32) for _ in range(4)])
        wkv = p.tile(es2, "wkv", [128, 16, 1024], BF16)
        memT = p.tile(es2, "memT", [128, 16, 256], BF16)
        load_w_cast(p, wkv, wkv_dram, 16)
        p.dma("sp", [(memT.t[:, :, :], xT_view(G["MEMT"]))], memT.b, reads=[G["MEMT_b"]], writes=[memT.b])
        for h in range(4):
            ps = psr.next()
            fns = [(lambda e, k=k, h=h, ps=ps: e.matmul(ps.t[:, 0:256], lhsT=wkv.t[:, k, h * 128:(h + 1) * 128],
                                                        rhs=memT.t[:, k, :], start=(k == 0), stop=(k == 15)))
                   for k in range(16)]
            p.compute("pe", fns, reads=[wkv.b, memT.b], writes=[ps.b])
            p.compute("act", lambda e, h=h, ps=ps: e.copy(out=kmT.t[:, h, :], in_=ps.t[:, 0:256]), reads=[ps.b], writes=[kmT.b])
        for mc in range(2):
            ps = psr.next()
            fns = [(lambda e, k=k, mc=mc, ps=ps: e.matmul(ps.t[:, :], lhsT=memT.t[:, k, mc * 128:(mc + 1) * 128],
                                                          rhs=wkv.t[:, k, 512:1024], start=(k == 0), stop=(k == 15)))
                   for k in range(16)]
            p.compute("pe", fns, reads=[wkv.b, memT.b], writes=[ps.b])
            p.compute("act", lambda e, mc=mc, ps=ps: e.copy(out=vm.t[:, mc, :], in_=ps.t[:, :]), reads=[ps.b], writes=[vm.b])
        p.flush()
    return kmT, vm


def emit_mem_attn_block(p, S, qmT, kmT, vm, ones, psr, mo):
    expT, rz = S["expT"], S["rz"]
    for h in range(4):
        pss = [psr.next(), psr.next()]
        for mc in range(2):
            p.compute("pe", lambda e, h=h, mc=mc, ps=pss[mc]: e.matmul(
                ps.t[:, :], lhsT=kmT.t[:, h, mc * 128:(mc + 1) * 128], rhs=qmT.t[:, h, :], start=True, stop=True),
                reads=[kmT.b, qmT.b], writes=[pss[mc].b])
            p.compute("act", lambda e, mc=mc, ps=pss[mc]: e.activation(out=expT.t[:, mc, :], in_=ps.t[:, :], func=AF.Exp, scale=ISQ),
                      reads=[pss[mc].b], writes=[expT.b])
        psn, psz = psr.next(), psr.next()
        p.compute("pe", [(lambda e, h=h, mc=mc, ps=psn: e.matmul(ps.t[:, :], lhsT=vm.t[:, mc, h * 128:(h + 1) * 128],
                                                               rhs=expT.t[:, mc, :], start=(mc == 0), stop=(mc == 1)))
                         for mc in range(2)], reads=[vm.b, expT.b], writes=[psn.b])
        p.compute("pe", [(lambda e, mc=mc, ps=psz: e.matmul(ps.t[:, :], lhsT=ones.t[:, :], rhs=expT.t[:, mc, :],
                                                          start=(mc == 0), stop=(mc == 1)))
                         for mc in range(2)], reads=[ones.b, expT.b], writes=[psz.b])
        p.compute("dve", lambda e, ps=psz: e.reciprocal(out=rz.t[:, :], in_=ps.t[:, :]), reads=[psz.b], writes=[rz.b])
        p.compute("dve", lambda e, h=h, ps=psn: e.tensor_tensor(out=mo.t[:, h, :], in0=ps.t[:, :], in1=rz.t[:, :], op=ALU.mult),
                  reads=[psn.b, rz.b], writes=[mo.b])


def phase_A1(p, G, l, ia, cur):
    nc = p.nc
    with ExitStack() as es:
        kmT, vm = emit_mem_kv(p, es, G["w_mem_kv"][l], G)
        psr = PsumRing([p.psum(es, "ps", [128, 512], F32) for _ in range(8)])
        ones = p.tile(es, "ones", [128, 128], BF16)
        p.dma("pool", [(ones.t[:, :], G["ones"])], ones.b, writes=[ones.b])
        win = p.tile(es, "win", [128, 16, 2048], BF16)
        load_w_cast(p, win, G["w_in_a"][ia], 16, sub=8)
        wpool = p.tile(es, "wpool", [128, 4, 3, 384], BF16)
        p.dma("pool", [(wpool.t[:, g, :, :], G["w_pool"][ia, g].rearrange("(k q) c -> q k c", q=128)) for g in range(4)],
              wpool.b, writes=[wpool.b])
        spool = p.tile(es, "spool", [128, 12], F32)
        p.dma("sp", [(spool.t[:, :], G["s_pool_l"][ia])], spool.b, writes=[spool.b])
        rc = p.tile(es, "rc", [128, 4, 512], F32)
        p.dma("sp", [(rc.t[:, :, :], G["rc"])], rc.b, writes=[rc.b])
        xTb = [p.tile(es, "xTb", [128, 16, 512], BF16) for _ in range(2)]
        pext = p.tile(es, "pext", [128, 12, 528], F32)
        pg = [Buf("pg%d" % g) for g in range(4)]
        tA = p.tile(es, "tA", [128, 3, 528], F32)
        tB = p.tile(es, "tB", [128, 3, 528], F32)
        diff = [p.tile(es, "diff", [128, 3, 512], BF16) for _ in range(2)]
        loc = [p.tile(es, "loc", [128, 3, 512], BF16) for _ in range(2)]
        qmT = p.tile(es, "qmT", [128, 4, 512], BF16)
        mo = [p.tile(es, "mo", [128, 4, 512], BF16) for _ in range(2)]
        S = {"expT": p.tile(es, "expT", [128, 2, 512], BF16), "rz": p.tile(es, "rz", [128, 512], F32)}
        p.compute("dve", lambda e: e.memset(pext.t[:, :, 0:16], 0.0), writes=pg)
        XTv = xT_view(G["XT"][cur])
        MIXv = xT_view(G["MIXT"])
        def load_xTb(bj):
            x_ = xTb[bj % 2]
            p.dma("sp", [(x_.t[:, :, :], XTv[:, :, bj * 512:(bj + 1) * 512])], x_.b, reads=[G["XT_b"][cur]], writes=[x_.b])

        load_xTb(0)
        for bi in range(NBLK):
            xT = xTb[bi % 2]
            if bi + 1 < NBLK:
                load_xTb(bi + 1)
            for j in range(16):
                ps = psr.next()
                fns = [(lambda e, k=k, j=j, ps=ps, xT=xT: e.matmul(ps.t[:, :], lhsT=win.t[:, k, j * 128:(j + 1) * 128],
                                                                   rhs=xT.t[:, k, :], start=(k == 0), stop=(k == 15)))
                       for k in range(16)]
                p.compute("pe", fns, reads=[win.b, xT.b], writes=[ps.b])
                if j < 12:
                    p.compute("act", lambda e, j=j, ps=ps: e.copy(out=pext.t[:, j, 16:528], in_=ps.t[:, :]),
                              reads=[ps.b], writes=[pg[j // 3]])
                else:
                    p.compute("act", lambda e, j=j, ps=ps: e.copy(out=qmT.t[:, j - 12, :], in_=ps.t[:, :]),
                              reads=[ps.b], writes=[qmT.b])
            mob = mo[bi % 2]
            emit_mem_attn_block(p, S, qmT, kmT, vm, ones, psr, mob)
            p.dma("sp", [(MIXv[:, 12:16, bi * 512:(bi + 1) * 512], mob.t[:, :, :])], mob.b, reads=[mob.b], writes=[G["MIXT_b"]])
            for g in range(4):
                w = 2 << g
                src = pext.t[:, 3 * g:3 * g + 3, :]
                lv = [tA, tB, tA, tB]
                off = 0
                prev = None
                for s in range(g + 1):
                    step = 1 << s
                    off += step
                    dst = lv[s]
                    if prev is None:
                        a0 = src[:, :, off:528]
                        a1 = src[:, :, off - step:528 - step]
                        rb = [pg[g]]
                    else:
                        a0 = prev.t[:, :, off:528]
                        a1 = prev.t[:, :, off - step:528 - step]
                        rb = [prev.b]
                    p.compute("dve", lambda e, dst=dst, a0=a0, a1=a1, off=off: e.tensor_tensor(
                        out=dst.t[:, :, off:528], in0=a0, in1=a1, op=ALU.add), reads=rb, writes=[dst.b])
                    prev = dst
                fin = prev
                df = diff[g % 2]
                if bi == 0:
                    p.compute("dve", lambda e, fin=fin, g=g: e.tensor_tensor(
                        out=fin.t[:, :, 16:528], in0=fin.t[:, :, 16:528],
                        in1=rc.t[:, g:g + 1, :].to_broadcast([128, 3, 512]), op=ALU.mult),
                        reads=[fin.b, rc.b], writes=[fin.b])
                    p.compute("dve", lambda e, fin=fin, df=df, src=src: e.tensor_tensor(
                        out=df.t[:, :, :], in0=fin.t[:, :, 16:528], in1=src[:, :, 16:528], op=ALU.subtract),
                        reads=[fin.b, pg[g]], writes=[df.b])
                else:
                    p.compute("dve", lambda e, fin=fin, df=df, src=src, w=w: e.scalar_tensor_tensor(
                        out=df.t[:, :, :], in0=fin.t[:, :, 16:528], scalar=1.0 / w, in1=src[:, :, 16:528],
                        op0=ALU.mult, op1=ALU.subtract), reads=[fin.b, pg[g]], writes=[df.b])
                p.compute("dve", lambda e, g=g: e.tensor_copy(out=pext.t[:, 3 * g:3 * g + 3, 0:16],
                                                               in_=pext.t[:, 3 * g:3 * g + 3, 512:528]),
                          reads=[pg[g]], writes=[pg[g]])
                lc = loc[g % 2]
                for oc in range(3):
                    ps = psr.next()
                    fns = [(lambda e, kc=kc, oc=oc, ps=ps, g=g, df=df: e.matmul(
                        ps.t[:, :], lhsT=wpool.t[:, g, kc, oc * 128:(oc + 1) * 128], rhs=df.t[:, kc, :],
                        start=(kc == 0), stop=(kc == 2))) for kc in range(3)]
                    p.compute("pe", fns, reads=[wpool.b, df.b], writes=[ps.b])
                    p.compute("act", lambda e, oc=oc, ps=ps, g=g, lc=lc: e.activation(
                        out=lc.t[:, oc, :], in_=ps.t[:, :], func=AF.Copy, scale=spool.t[:, 3 * g + oc:3 * g + oc + 1]),
                        reads=[ps.b, spool.b], writes=[lc.b])
                p.dma("sp", [(MIXv[:, 3 * g:3 * g + 3, bi * 512:(bi + 1) * 512], lc.t[:, :, :])], lc.b,
                      reads=[lc.b], writes=[G["MIXT_b"]])
        p.flush()


def phase_O(p, G, l, cur, which=0):
    nxt = 1 - cur
    with ExitStack() as es:
        psr = PsumRing([p.psum(es, "ps", [128, 512], F32) for _ in range(6)])
        pst = PsumRing([p.psum(es, "pst", [128, 1024], BF16) for _ in range(2)])
        ident = p.tile(es, "ident", [128, 128], BF16)
        p.dma("pool", [(ident.t[:, :], G["ident"])], ident.b, writes=[ident.b])
        if which == 0:
            wo = p.tile(es, "wo", [128, 16, 2048], BF16)
            load_w_cast(p, wo, G["w_o"][l], 16, sub=8)
            mixb = [p.tile(es, "mixb", [128, 16, 512], BF16) for _ in range(2)]
        else:
            yin = [p.tile(es, "yin", [128, D], F32) for _ in range(2)]
        lng = p.tile(es, "lng", [128, D], F32)
        lnb = p.tile(es, "lnb", [128, D], F32)
        p.dma("sp", [(lng.t[:, :], G["ln_g"][l, which:which + 1, :].to_broadcast([128, D]))], lng.b, writes=[lng.b])
        p.dma("sp", [(lnb.t[:, :], G["ln_b"][l, which:which + 1, :].to_broadcast([128, D]))], lnb.b, writes=[lnb.b])
        S = {"xb": p.tile(es, "xb", [128, D], BF16), "junk": p.tile(es, "junk", [128, D], BF16),
             "ln_mv": p.tile(es, "lnmv", [128, 2], F32), "ln_rstd": p.tile(es, "lnrs", [128, 1], F32),
             "eps": p.tile(es, "eps", [128, 1], F32)}
        p.compute("dve", lambda e: e.memset(S["eps"].t[:, :], EPS), writes=[S["eps"].b])
        xin = [p.tile(es, "xin", [128, D], F32) for _ in range(2)]
        rr = [p.tile(es, "rr", [128, D], F32) for _ in range(2)]
        xn = [p.tile(es, "xn", [128, D], F32) for _ in range(2)]
        xTo = [p.tile(es, "xTo", [128, 16, 512], BF16) for _ in range(2)]
        MIXv = xT_view(G["MIXT"])

        def load_xy(tj):
            x_ = xin[tj % 2]
            p.dma("sp", [(x_.t[:, :], G["X"][cur][tj * 128:(tj + 1) * 128, :])], x_.b, reads=[G["X_b"][cur]], writes=[x_.b])
            if which == 1:
                y_ = yin[tj % 2]
                p.dma("sp", [(y_.t[:, :], G["Y"][tj * 128:(tj + 1) * 128, :])], y_.b, reads=[G["Y_b"]], writes=[y_.b])

        for bi in range(NBLK):
            if which == 0:
                mb = mixb[bi % 2]
                if bi == 0:
                    p.dma("sp", [(mb.t[:, :, :], MIXv[:, :, 0:512])], mb.b, reads=[G["MIXT_b"]], writes=[mb.b])
                if bi + 1 < NBLK:
                    mb2 = mixb[(bi + 1) % 2]
                    p.dma("sp", [(mb2.t[:, :, :], MIXv[:, :, (bi + 1) * 512:(bi + 2) * 512])], mb2.b, reads=[G["MIXT_b"]], writes=[mb2.b])
            xo = xTo[bi % 2]
            for tt in range(4):
                ti = bi * 4 + tt
                xt = xin[ti % 2]
                if ti == 0:
                    load_xy(0)
                if ti + 1 < NT:
                    load_xy(ti + 1)
                r = rr[ti % 2]
                if which == 1:
                    yt = yin[ti % 2]
                    p.compute("dve", lambda e, xt=xt, r=r, yt=yt: e.scalar_tensor_tensor(
                        out=r.t[:, :], in0=xt.t[:, :], scalar=ALPHA, in1=yt.t[:, :], op0=ALU.mult, op1=ALU.add),
                        reads=[yt.b, xt.b], writes=[r.b])
                for cb in (range(4) if which == 0 else ()):
                    ps = psr.next()
                    fns = [(lambda e, k=k, cb=cb, ps=ps, tt=tt, mb=mb: e.matmul(
                        ps.t[:, :], lhsT=mb.t[:, k, tt * 128:(tt + 1) * 128], rhs=wo.t[:, k, cb * 512:(cb + 1) * 512],
                        start=(k == 0), stop=(k == 15))) for k in range(16)]
                    p.compute("pe", fns, reads=[wo.b, mb.b], writes=[ps.b])
                    p.compute("dve", lambda e, cb=cb, ps=ps, xt=xt, r=r: e.scalar_tensor_tensor(
                        out=r.t[:, cb * 512:(cb + 1) * 512], in0=xt.t[:, cb * 512:(cb + 1) * 512], scalar=ALPHA,
                        in1=ps.t[:, :], op0=ALU.mult, op1=ALU.add), reads=[ps.b, xt.b], writes=[r.b])
                x2 = xn[ti % 2]
                emit_ln_tile(p, S, r, x2, lng, lnb)
                p.dma("sp", [(G["X"][nxt][ti * 128:(ti + 1) * 128, :], x2.t[:, :])], x2.b, reads=[x2.b], writes=[G["X_b"][nxt]])
                emit_transpose_tile(p, S, x2, xo, tt, ident, pst)
            p.dma("sp", [(xT_view(G["XT"][nxt])[:, :, bi * 512:(bi + 1) * 512], xo.t[:, :, :])], xo.b,
                  reads=[xo.b], writes=[G["XT_b"][nxt]])
        p.flush()


def phase_B1(p, G, l, ib, cur):
    XTv = xT_view(G["XT"][cur])
    wsrc = G["w_in_b"][ib]
    with ExitStack() as es:
        psr = PsumRing([p.psum(es, "ps", [128, 512], F32) for _ in range(8)])
        w = p.tile(es, "wqk", [128, 16, 3072], BF16)
        load_w_cast(p, w, wsrc[:, 0:3072], 16, sub=16)
        xTb = [p.tile(es, "xTb", [128, 16, 512], BF16) for _ in range(2)]
        ob = [p.tile(es, "ob", [128, 24, 512], BF16) for _ in range(2)]
        QKv = xT_view(G["QKT"])
        def load_xTb(bj):
            x_ = xTb[bj % 2]
            p.dma("sp", [(x_.t[:, :, :], XTv[:, :, bj * 512:(bj + 1) * 512])], x_.b, reads=[G["XT_b"][cur]], writes=[x_.b])

        load_xTb(0)
        for bi in range(NBLK):
            xT = xTb[bi % 2]
            if bi + 1 < NBLK:
                load_xTb(bi + 1)
            o_ = ob[bi % 2]
            for j in range(24):
                ps = psr.next()
                fns = [(lambda e, k=k, j=j, ps=ps, xT=xT: e.matmul(ps.t[:, :], lhsT=w.t[:, k, j * 128:(j + 1) * 128],
                                                                   rhs=xT.t[:, k, :], start=(k == 0), stop=(k == 15)))
                       for k in range(16)]
                p.compute("pe", fns, reads=[w.b, xT.b], writes=[ps.b])
                if j % 2 == 0:
                    p.compute("act", lambda e, j=j, ps=ps, o_=o_: e.copy(out=o_.t[:, j, :], in_=ps.t[:, :]), reads=[ps.b], writes=[o_.b])
                else:
                    p.compute("dve", lambda e, j=j, ps=ps, o_=o_: e.tensor_copy(out=o_.t[:, j, :], in_=ps.t[:, :]), reads=[ps.b], writes=[o_.b])
            p.dma("sp", [(QKv[:, 0:12, bi * 512:(bi + 1) * 512], o_.t[:, 0:12, :]), (QKv[:, 12:24, bi * 512:(bi + 1) * 512], o_.t[:, 12:24, :])],
                  o_.b, reads=[o_.b], writes=[G["QKT_b"]])
        p.flush()
    with ExitStack() as es:
        kmT, vm = emit_mem_kv(p, es, G["w_mem_kv"][l], G)
        psr = PsumRing([p.psum(es, "ps", [128, 512], F32) for _ in range(8)])
        ones = p.tile(es, "ones", [128, 128], BF16)
        p.dma("pool", [(ones.t[:, :], G["ones"])], ones.b, writes=[ones.b])
        w = p.tile(es, "wvq", [128, 16, 2048], BF16)
        load_w_cast(p, w, wsrc[:, 3072:5120], 16, sub=16)
        xTb = [p.tile(es, "xTb", [128, 16, 512], BF16) for _ in range(2)]
        vst = [p.tile(es, "vst", [128, 1536], BF16) for _ in range(2)]
        qmT = p.tile(es, "qmT", [128, 4, 512], BF16)
        mo = [p.tile(es, "mo", [128, 4, 512], BF16) for _ in range(2)]
        S = {"expT": p.tile(es, "expT", [128, 2, 512], BF16), "rz": p.tile(es, "rz", [128, 512], F32)}
        MIXv = xT_view(G["MIXT"])
        def load_xTb(bj):
            x_ = xTb[bj % 2]
            p.dma("sp", [(x_.t[:, :, :], XTv[:, :, bj * 512:(bj + 1) * 512])], x_.b, reads=[G["XT_b"][cur]], writes=[x_.b])

        load_xTb(0)
        for bi in range(NBLK):
            xT = xTb[bi % 2]
            if bi + 1 < NBLK:
                load_xTb(bi + 1)
            for j in range(4):
                ps = psr.next()
                fns = [(lambda e, k=k, j=j, ps=ps, xT=xT: e.matmul(ps.t[:, :], lhsT=w.t[:, k, 1536 + j * 128:1536 + (j + 1) * 128],
                                                                   rhs=xT.t[:, k, :], start=(k == 0), stop=(k == 15)))
                       for k in range(16)]
                p.compute("pe", fns, reads=[w.b, xT.b], writes=[ps.b])
                p.compute("act", lambda e, j=j, ps=ps: e.copy(out=qmT.t[:, j, :], in_=ps.t[:, :]), reads=[ps.b], writes=[qmT.b])
            mob = mo[bi % 2]
            emit_mem_attn_block(p, S, qmT, kmT, vm, ones, psr, mob)
            p.dma("sp", [(MIXv[:, 12:16, bi * 512:(bi + 1) * 512], mob.t[:, :, :])], mob.b, reads=[mob.b], writes=[G["MIXT_b"]])
            for tt in range(4):
                ti = bi * 4 + tt
                vs = vst[ti % 2]
                for cb in range(3):
                    ps = psr.next()
                    fns = [(lambda e, k=k, cb=cb, ps=ps, xT=xT, tt=tt: e.matmul(
                        ps.t[:, :], lhsT=xT.t[:, k, tt * 128:(tt + 1) * 128], rhs=w.t[:, k, cb * 512:(cb + 1) * 512],
                        start=(k == 0), stop=(k == 15))) for k in range(16)]
                    p.compute("pe", fns, reads=[w.b, xT.b], writes=[ps.b])
                    if cb % 2 == 0:
                        p.compute("act", lambda e, cb=cb, ps=ps, vs=vs: e.copy(out=vs.t[:, cb * 512:(cb + 1) * 512], in_=ps.t[:, :]),
                                  reads=[ps.b], writes=[vs.b])
                    else:
                        p.compute("dve", lambda e, cb=cb, ps=ps, vs=vs: e.tensor_copy(out=vs.t[:, cb * 512:(cb + 1) * 512], in_=ps.t[:, :]),
                                  reads=[ps.b], writes=[vs.b])
                p.dma("sp", [(G["V"][ti * 128:(ti + 1) * 128, :], vs.t[:, :])], vs.b, reads=[vs.b], writes=[G["V_b"]])
        p.flush()


def phase_B2(p, G, l, cur):
    DILS = (1, 4, 16)
    with ExitStack() as es:
        psS = [p.psum(es, "psS", [128, 256], F32) for _ in range(2)]
        psN = [p.psum(es, "psN", [128, 128], F32) for _ in range(2)]
        psZ = [p.psum(es, "psZ", [128, 128], F32) for _ in range(2)]
        ones = p.tile(es, "ones", [128, 128], BF16)
        p.dma("pool", [(ones.t[:, :], G["ones"])], ones.b, writes=[ones.b])
        masks = p.tile(es, "masks", [128, 2, 128], BF16)
        p.dma("pool", [(masks.t[:, :, :], G["masks"])], masks.b, writes=[masks.b])
        qT = [p.tile(es, "qT", [128, T], BF16) for _ in range(3)]
        kT = [p.tile(es, "kT", [128, T], BF16) for _ in range(3)]
        vv = [p.tile(es, "vv", [128, 32, 128], BF16) for _ in range(3)]
        num = [p.tile(es, "num", [128, T], F32) for _ in range(3)]
        zacc = p.tile(es, "zacc", [128, T], F32)
        ob = [p.tile(es, "ob", [128, T], BF16) for _ in range(2)]
        ex = [p.tile(es, "ex", [128, 2, 128], BF16) for _ in range(2)]
        cnt = 0
        for h in range(4):
            for g in range(3):
                d = DILS[g]
                nb = T // (128 * d)
                r0 = g * 512 + h * 128
                p.dma("sp", [(qT[g].t[:, :], G["QKT"][r0:r0 + 128, :])], qT[g].b, reads=[G["QKT_b"]], writes=[qT[g].b])
                p.dma("sp", [(kT[g].t[:, :], G["QKT"][1536 + r0:1536 + r0 + 128, :])], kT[g].b, reads=[G["QKT_b"]], writes=[kT[g].b])
                Vv = G["V"][:, r0:r0 + 128].rearrange("(b i r) c -> i r b c", i=128, r=d)
                vt = vv[g].t[:, :, :].rearrange("i (r b) c -> i r b c", r=d)
                parts = max(1, 4 // d)
                bs = nb // parts
                pairs = []
                for r in range(d):
                    for pb in range(parts):
                        pairs.append((vt[:, r, pb * bs:(pb + 1) * bs, :], Vv[:, r, pb * bs:(pb + 1) * bs, :]))
                p.dma("sp", pairs, vv[g].b, reads=[G["V_b"]], writes=[vv[g].b])
            for g in range(3):
                d = DILS[g]
                nb = T // (128 * d)
                qv = qT[g].t[:, :].rearrange("q (b i r) -> q r b i", i=128, r=d)
                kv = kT[g].t[:, :].rearrange("q (b i r) -> q r b i", i=128, r=d)
                nv = num[g].t[:, :].rearrange("q (b i r) -> q r b i", i=128, r=d)
                zv = zacc.t[:, :].rearrange("q (b i r) -> q r b i", i=128, r=d)
                vt = vv[g].t[:, :, :].rearrange("i (r b) c -> i r b c", r=d)
                for r in range(d):
                    for ub in range(nb):
                        ps, pn, pz, ex_ = psS[cnt % 2], psN[cnt % 2], psZ[cnt % 2], ex[cnt % 2]
                        cnt += 1
                        blks = ([(0, ub - 1)] if ub > 0 else []) + [(1, ub)]
                        fns = [(lambda e, m=m, kb=kb, ps=ps, kv=kv, qv=qv, r=r, ub=ub: e.matmul(
                            ps.t[:, m * 128:(m + 1) * 128], lhsT=kv[:, r, kb, :], rhs=qv[:, r, ub, :], start=True, stop=True))
                            for (m, kb) in blks]
                        p.compute("pe", fns, reads=[kT[g].b, qT[g].b], writes=[ps.b])
                        lo = blks[0][0]
                        p.compute("act", lambda e, ps=ps, ex_=ex_, lo=lo: e.activation(
                            out=ex_.t[:, lo:2, :], in_=ps.t[:, lo * 128:256].rearrange("q (m t) -> q m t", t=128), func=AF.Exp, scale=ISQ),
                            reads=[ps.b], writes=[ex_.b])
                        p.compute("dve", lambda e, ex_=ex_, lo=lo: e.tensor_tensor(out=ex_.t[:, lo:2, :], in0=ex_.t[:, lo:2, :],
                                                                                 in1=masks.t[:, lo:2, :], op=ALU.mult),
                                  reads=[ex_.b, masks.b], writes=[ex_.b])
                        nblk = len(blks)
                        fns = [(lambda e, i=i, m=m, kb=kb, pn=pn, vt=vt, ex_=ex_, r=r, nblk=nblk: e.matmul(
                            pn.t[:, :], lhsT=vt[:, r, kb, :], rhs=ex_.t[:, m, :], start=(i == 0), stop=(i == nblk - 1)))
                            for i, (m, kb) in enumerate(blks)]
                        fns += [(lambda e, i=i, m=m, pz=pz, ex_=ex_, nblk=nblk: e.matmul(
                            pz.t[:, :], lhsT=ones.t[:, :], rhs=ex_.t[:, m, :], start=(i == 0), stop=(i == nblk - 1)))
                            for i, (m, kb) in enumerate(blks)]
                        p.compute("pe", fns, reads=[vv[g].b, ex_.b, ones.b], writes=[pn.b, pz.b])
                        p.compute("act", lambda e, pn=pn, nv=nv, r=r, ub=ub: e.copy(out=nv[:, r, ub, :], in_=pn.t[:, :]),
                                  reads=[pn.b], writes=[num[g].b])
                        if g == 0:
                            p.compute("dve", lambda e, pz=pz, zv=zv, r=r, ub=ub: e.tensor_copy(out=zv[:, r, ub, :], in_=pz.t[:, :]),
                                      reads=[pz.b], writes=[zacc.b])
                        else:
                            p.compute("dve", lambda e, pz=pz, zv=zv, r=r, ub=ub: e.tensor_tensor(
                                out=zv[:, r, ub, :], in0=zv[:, r, ub, :], in1=pz.t[:, :], op=ALU.add), reads=[pz.b, zacc.b], writes=[zacc.b])
            p.compute("dve", lambda e: e.reciprocal(out=zacc.t[:, :], in_=zacc.t[:, :]), reads=[zacc.b], writes=[zacc.b])
            for g in range(3):
                o_ = ob[(h * 3 + g) % 2]
                eng = "dve" if g != 1 else "pool"
                p.compute(eng, lambda e, o_=o_, g=g: e.tensor_tensor(out=o_.t[:, :], in0=num[g].t[:, :], in1=zacc.t[:, :], op=ALU.mult),
                          reads=[num[g].b, zacc.b], writes=[o_.b])
                r0 = g * 512 + h * 128
                p.dma("sp", [(G["MIXT"][r0:r0 + 128, :], o_.t[:, :])], o_.b, reads=[o_.b], writes=[G["MIXT_b"]])
        p.flush()


def phase_Q(p, G, l, cur):
    NEGB = -1.0e30
    with ExitStack() as es:
        psr = PsumRing([p.psum(es, "ps", [128, 512], F32) for _ in range(8)])
        wq = p.tile(es, "wq", [128, 16, 2048], BF16)
        load_w_cast(p, wq, G["peer_wq"][l], 16, sub=8)
        keysT = p.tile(es, "keysT", [128, 2048], BF16)
        p.dma("pool", [(keysT.t[:, :], G["keysT"][l])], keysT.b, writes=[keysT.b])
        cin = [p.tile(es, "cin", [128, 2048], F32) for _ in range(2)]
        cout = [p.tile(es, "cout", [128, 2048], BF16) for _ in range(2)]
        cast_units = []
        if "UB" in G:
            uT = G["peer_uT"][l].rearrange("(k q) e -> q k e", q=128)
            for g in range(NG):
                for hf in range(4):
                    cast_units.append((uT[:, hf * 4:(hf + 1) * 4, g * EG:(g + 1) * EG], G["UB"][g][:, hf * 4:(hf + 1) * 4, :],
                                       "q (k e) -> q k e", 4, G["UB_b"]))
                    cast_units.append((G["peer_v"][l][g * EG + hf * 128:g * EG + (hf + 1) * 128, :].rearrange("(c q) d -> q c d", q=128),
                                       G["VB"][g][:, hf:hf + 1, :], "q (k e) -> q k e", 1, G["VB_b"]))
        cast_state = {"i": 0}

        def emit_cast(n):
            for _ in range(n):
                i = cast_state["i"]
                if i >= len(cast_units):
                    return
                cast_state["i"] = i + 1
                src, dst, pat, kk, db = cast_units[i]
                ci, co = cin[i % 2], cout[i % 2]
                p.dma("sp", [(ci.t[:, :].rearrange(pat, k=kk), src)], ci.b, writes=[ci.b])
                p.compute("pool", lambda e, ci=ci, co=co: e.tensor_copy(out=co.t[:, :], in_=ci.t[:, :]), reads=[ci.b], writes=[co.b])
                p.dma("sp", [(dst, co.t[:, :].rearrange(pat, k=kk))], co.b, reads=[co.b], writes=[db])

        xTb = [p.tile(es, "xTb", [128, 16, 512], BF16) for _ in range(2)]
        qT = p.tile(es, "qT", [128, 16, 512], BF16)
        Sx = [p.tile(es, "Sx", [128, 2048], F32) for _ in range(2)]
        work = p.tile(es, "work", [128, 128], F32)
        top = p.tile(es, "top", [128, 8, 2, 16], F32)
        cand = p.tile(es, "cand", [128, 8, 256], F32)
        cwork = p.tile(es, "cwork", [128, 256], F32)
        ctop = p.tile(es, "ctop", [128, 8, 16], F32)
        cex = p.tile(es, "cex", [128, 8, 16], F32)
        zs = p.tile(es, "zs", [128, 8], F32)
        tn = [p.tile(es, "tn", [128, 16], F32) for _ in range(2)]
        XTv = xT_view(G["XT"][cur])
        def load_xTb(bj):
            x_ = xTb[bj % 2]
            p.dma("sp", [(x_.t[:, :, :], XTv[:, :, bj * 512:(bj + 1) * 512])], x_.b, reads=[G["XT_b"][cur]], writes=[x_.b])

        load_xTb(0)
        for bi in range(NBLK):
            xT = xTb[bi % 2]
            if bi + 1 < NBLK:
                load_xTb(bi + 1)
            for j in range(16):
                ps = psr.next()
                fns = [(lambda e, k=k, j=j, ps=ps, xT=xT: e.matmul(ps.t[:, :], lhsT=wq.t[:, k, j * 128:(j + 1) * 128],
                                                                   rhs=xT.t[:, k, :], start=(k == 0), stop=(k == 15)))
                       for k in range(16)]
                p.compute("pe", fns, reads=[wq.b, xT.b], writes=[ps.b])
                p.compute("act", lambda e, j=j, ps=ps: e.copy(out=qT.t[:, j, :], in_=ps.t[:, :]), reads=[ps.b], writes=[qT.b])
            for tt in range(4):
                ti = bi * 4 + tt
                S_ = Sx[ti % 2]
                emit_cast(8)
                for bk in range(4):
                    ps = psr.next()
                    fns = [(lambda e, j=j, ps=ps, tt=tt: e.matmul(ps.t[:, (j % 4) * 128:(j % 4 + 1) * 128],
                                                                  lhsT=qT.t[:, j, tt * 128:(tt + 1) * 128],
                                                                  rhs=keysT.t[:, j * 128:(j + 1) * 128], start=True, stop=True))
                           for j in range(bk * 4, bk * 4 + 4)]
                    p.compute("pe", fns, reads=[qT.b, keysT.b], writes=[ps.b])
                    p.compute("act", lambda e, bk=bk, ps=ps, S_=S_: e.copy(out=S_.t[:, bk * 512:(bk + 1) * 512], in_=ps.t[:, :]),
                              reads=[ps.b], writes=[S_.b])
                p.dma("sp", [(G["SALL"][ti * 128:(ti + 1) * 128, :], S_.t[:, :])], S_.b, reads=[S_.b], writes=[G["SALL_b"]])
                fns = []
                for j in range(16):
                    h, pp = j // 2, j % 2
                    sj = S_.t[:, j * 128:(j + 1) * 128]
                    fns.append(lambda e, sj=sj, h=h, pp=pp: e.max(out=top.t[:, h, pp, 0:8], in_=sj))
                    fns.append(lambda e, sj=sj, h=h, pp=pp: e.match_replace(out=work.t[:, :], in_to_replace=top.t[:, h, pp, 0:8],
                                                                          in_values=sj, imm_value=NEGB))
                    fns.append(lambda e, h=h, pp=pp: e.max(out=top.t[:, h, pp, 8:16], in_=work.t[:, :]))
                for f in fns:
                    p.compute("dve", f, reads=[S_.b, top.b, work.b], writes=[top.b, work.b])
                p.compute("dve", lambda e: e.tensor_tensor(
                    out=cand.t[:, :, :].rearrange("q h (a b) -> q h a b", a=16),
                    in0=top.t[:, :, 0, :].unsqueeze(3).to_broadcast([128, 8, 16, 16]),
                    in1=top.t[:, :, 1, :].unsqueeze(2).to_broadcast([128, 8, 16, 16]), op=ALU.add),
                    reads=[top.b], writes=[cand.b])
                fns = []
                for h in range(8):
                    fns.append(lambda e, h=h: e.max(out=ctop.t[:, h, 0:8], in_=cand.t[:, h, :]))
                    fns.append(lambda e, h=h: e.match_replace(out=cwork.t[:, :], in_to_replace=ctop.t[:, h, 0:8],
                                                              in_values=cand.t[:, h, :], imm_value=NEGB))
                    fns.append(lambda e, h=h: e.max(out=ctop.t[:, h, 8:16], in_=cwork.t[:, :]))
                for f in fns:
                    p.compute("dve", f, reads=[cand.b, ctop.b, cwork.b], writes=[ctop.b, cwork.b])
                tn_ = tn[ti % 2]
                p.compute("dve", lambda e: e.tensor_tensor(out=cex.t[:, :, :], in0=ctop.t[:, :, :],
                                                           in1=ctop.t[:, :, 0:1].to_broadcast([128, 8, 16]), op=ALU.subtract),
                          reads=[ctop.b], writes=[cex.b])
                p.compute("act", lambda e: e.activation(out=cex.t[:, :, :], in_=cex.t[:, :, :], func=AF.Exp), reads=[cex.b], writes=[cex.b])
                p.compute("dve", lambda e: e.reduce_sum(out=zs.t[:, :], in_=cex.t[:, :, :], axis=mybir.AxisListType.X),
                          reads=[cex.b], writes=[zs.b])
                p.compute("act", lambda e: e.activation(out=zs.t[:, :], in_=zs.t[:, :], func=AF.Ln), reads=[zs.b], writes=[zs.b])
                p.compute("dve", lambda e, tn_=tn_: e.scalar_tensor_tensor(
                    out=tn_.t[:, 8:16], in0=ctop.t[:, :, 0], scalar=-1.0, in1=zs.t[:, :], op0=ALU.mult, op1=ALU.subtract),
                    reads=[ctop.b, zs.b], writes=[tn_.b])
                p.compute("dve", lambda e, tn_=tn_: e.tensor_copy(out=tn_.t[:, 0:8], in_=ctop.t[:, :, 15]), reads=[ctop.b], writes=[tn_.b])
                p.dma("sp", [(G["TN"][ti * 128:(ti + 1) * 128, :], tn_.t[:, :])], tn_.b, reads=[tn_.b], writes=[G["TN_b"]])
        p.flush()


def phase_PEER(p, G, l, cur):
    NGT = NBLK * NG
    with ExitStack() as es:
        psA = [p.psum(es, "psA", [128, 512], F32) for _ in range(2)]
        psG = [p.psum(es, "psG", [128, 512], F32) for _ in range(2)]
        psY = p.psum(es, "psY", [128, 2048], F32)
        ident = p.tile(es, "ident", [128, 128], BF16)
        p.dma("pool", [(ident.t[:, :], G["ident"])], ident.b, writes=[ident.b])
        xT = p.tile(es, "xT", [128, 16, 512], BF16)
        Sx = [p.tile(es, "Sx", [128, 8, 2, 128], F32) for _ in range(4)]
        tn = [p.tile(es, "tn", [128, 16], F32) for _ in range(4)]
        thn = [p.tile(es, "thn", [128, 8], F32) for _ in range(4)]
        yacc = [p.tile(es, "yacc", [128, 2048], F32) for _ in range(4)]
        ug = [p.tile(es, "ug", [128, 16, EG], BF16) for _ in range(2)]
        vg = [p.tile(es, "vg", [128, 4, 2048], BF16) for _ in range(2)]
        zz = [p.tile(es, "zz", [128, 8, 4, 128], F32) for _ in range(2)]
        zzB = [Buf("zzB0"), Buf("zzB1")]
        ee = [p.tile(es, "ee", [128, 8, 4, 128], BF16) for _ in range(2)]
        aS = [p.tile(es, "aS", [128, 4, 512], BF16) for _ in range(2)]
        WT = [p.tile(es, "WT", [128, 4, 128], BF16) for _ in range(2)]
        XTv = xT_view(G["XT"][cur])
        HP = 3

        def load_u(gi):
            u_ = ug[gi % 2]
            p.dma("sp", [(u_.t[:, :, :], G["UB"][gi % NG])], u_.b, reads=[G["UB_b"]], writes=[u_.b])

        def load_v(gi):
            v_ = vg[gi % 2]
            p.dma("sp", [(v_.t[:, :, :], G["VB"][gi % NG])], v_.b, reads=[G["VB_b"]], writes=[v_.b])

        def load_xT(bi):
            p.dma("sp", [(xT.t[:, :, :], XTv[:, :, bi * 512:(bi + 1) * 512])], xT.b, reads=[G["XT_b"][cur]], writes=[xT.b])

        acnt = [0]

        def emit_aT(gi, c):
            pa = psA[acnt[0] % 2]
            acnt[0] += 1
            u_, a_ = ug[gi % 2], aS[gi % 2]
            fns = [(lambda e, k=k, c=c, pa=pa, u_=u_: e.matmul(pa.t[:, :], lhsT=u_.t[:, k, c * 128:(c + 1) * 128],
                                                              rhs=xT.t[:, k, :], start=(k == 0), stop=(k == 15)))
                   for k in range(16)]
            p.compute("pe", fns, reads=[xT.b, u_.b], writes=[pa.b])
            p.compute("act", lambda e, pa=pa, a_=a_, c=c: e.copy(out=a_.t[:, c, :], in_=pa.t[:, :]), reads=[pa.b], writes=[a_.b])

        def front(g, tt, n):
            pb = n % 2
            z_, e_, S_ = zz[pb], ee[pb], Sx[tt]
            p.compute("pool", lambda e, S_=S_, g=g, z_=z_: e.tensor_tensor(
                out=z_.t[:, 0:HP, :, :],
                in0=S_.t[:, 0:HP, 0, g * 4:(g + 1) * 4].unsqueeze(3).to_broadcast([128, HP, 4, 128]),
                in1=S_.t[:, 0:HP, 1, :].unsqueeze(2).to_broadcast([128, HP, 4, 128]), op=ALU.add),
                reads=[S_.b], writes=[z_.b])
            p.compute("dve", lambda e, S_=S_, g=g, z_=z_: e.tensor_tensor(
                out=z_.t[:, HP:8, :, :],
                in0=S_.t[:, HP:8, 0, g * 4:(g + 1) * 4].unsqueeze(3).to_broadcast([128, 8 - HP, 4, 128]),
                in1=S_.t[:, HP:8, 1, :].unsqueeze(2).to_broadcast([128, 8 - HP, 4, 128]), op=ALU.add),
                reads=[S_.b], writes=[zzB[pb]])
            p.compute("act", lambda e, z_=z_, e_=e_: e.activation(out=e_.t[:, :, :, :], in_=z_.t[:, :, :, :], func=AF.Exp),
                      reads=[z_.b, zzB[pb]], writes=[e_.b])

        def stageB(gi, tt, n):
            pb = n % 2
            pg, z_, e_ = psG[pb], zz[pb], ee[pb]
            th_ = thn[tt]
            p.compute("dve", [(lambda e, h=h, th_=th_, z_=z_, e_=e_: e.scalar_tensor_tensor(
                out=e_.t[:, h, :, :], in0=z_.t[:, h, :, :], scalar=th_.t[:, h:h + 1], in1=e_.t[:, h, :, :],
                op0=ALU.is_ge, op1=ALU.mult)) for h in range(8)], reads=[z_.b, zzB[pb], e_.b, th_.b], writes=[e_.b])
            fns = [(lambda e, c=c, h=h, pg=pg, e_=e_: e.matmul(pg.t[:, c * 128:(c + 1) * 128], lhsT=e_.t[:, h, c, :], rhs=ident.t[:, :],
                                                              start=(h == 0), stop=(h == 7))) for c in range(4) for h in range(8)]
            p.compute("pe", fns, reads=[e_.b, ident.b], writes=[pg.b])

        def stageC(gi, tt, n):
            pb = n % 2
            v_, a_ = vg[gi % 2], aS[gi % 2]
            pg, W_ = psG[pb], WT[pb]
            p.compute("dve", lambda e, pg=pg, a_=a_, W_=W_, tt=tt: e.tensor_tensor(
                out=W_.t[:, :, :], in0=pg.t[:, :].rearrange("q (c t) -> q c t", c=4), in1=a_.t[:, :, tt * 128:(tt + 1) * 128],
                op=ALU.mult), reads=[pg.b, a_.b], writes=[W_.b])
            fns = [(lambda e, c=c, db=db, v_=v_, W_=W_: e.matmul(psY.t[:, db * 512:(db + 1) * 512], lhsT=W_.t[:, c, :],
                                                                rhs=v_.t[:, c, db * 512:(db + 1) * 512], start=(c == 0), stop=(c == 3)))
                   for db in range(4) for c in range(4)]
            p.compute("pe", fns, reads=[W_.b, v_.b], writes=[psY.b])

        def stageD(gi, tt, n):
            g = gi % NG
            ya = yacc[tt]
            if g == 0:
                p.compute("dve", lambda e, ya=ya: e.tensor_copy(out=ya.t[:, :], in_=psY.t[:, :]), reads=[psY.b], writes=[ya.b])
            else:
                p.compute("dve", lambda e, ya=ya: e.tensor_tensor(out=ya.t[:, :], in0=ya.t[:, :], in1=psY.t[:, :], op=ALU.add),
                          reads=[psY.b, ya.b], writes=[ya.b])

        load_u(0)
        load_u(1)
        load_v(0)
        load_xT(0)
        for tt in range(4):
            emit_aT(0, tt)
        n = 0
        for bi in range(NBLK):
            for tt in range(4):
                ti = bi * 4 + tt
                S_, tn_ = Sx[tt], tn[tt]
                p.dma("sp", [(S_.t[:, :, :, :], G["SALL"][ti * 128:(ti + 1) * 128, :].rearrange("q (h s n) -> q h s n", h=8, s=2))],
                      S_.b, reads=[G["SALL_b"]], writes=[S_.b])
                p.dma("sp", [(tn_.t[:, :], G["TN"][ti * 128:(ti + 1) * 128, :])], tn_.b, reads=[G["TN_b"]], writes=[tn_.b])
                p.compute("dve", lambda e, S_=S_, tn_=tn_: e.tensor_tensor(
                    out=S_.t[:, :, 0, :], in0=S_.t[:, :, 0, :], in1=tn_.t[:, 8:16].unsqueeze(2).to_broadcast([128, 8, 128]), op=ALU.add),
                    reads=[S_.b, tn_.b], writes=[S_.b])
                p.compute("dve", lambda e, tn_=tn_, th_=thn[tt]: e.tensor_tensor(out=th_.t[:, :], in0=tn_.t[:, 0:8], in1=tn_.t[:, 8:16], op=ALU.add),
                          reads=[tn_.b], writes=[thn[tt].b])
            units = [(g, tt) for g in range(NG) for tt in range(4)]
            L = len(units)
            base = bi * L
            for i in range(-2, L + 1):
                if 0 <= i - 1 < L:
                    g, tt = units[i - 1]
                    stageD(bi * NG + g, tt, base + i - 1)
                if 0 <= i < L:
                    g, tt = units[i]
                    gi = bi * NG + g
                    if tt == 0:
                        if gi + 2 < NGT:
                            load_u(gi + 2)
                        if gi + 1 < NGT:
                            load_v(gi + 1)
                        if g == NG - 1 and bi + 1 < NBLK:
                            load_xT(bi + 1)
                        a_ = aS[gi % 2]
                        p.compute("act", lambda e, a_=a_: e.activation(out=a_.t[:, :, :], in_=a_.t[:, :, :], func=AF.Gelu),
                                  reads=[a_.b], writes=[a_.b])
                    stageC(gi, tt, base + i)
                    if gi + 1 < NGT:
                        emit_aT(gi + 1, tt)
                if 0 <= i + 1 < L:
                    g, tt = units[i + 1]
                    stageB(bi * NG + g, tt, base + i + 1)
                if 0 <= i + 2 < L:
                    g, tt = units[i + 2]
                    front(g, tt, base + i + 2)
            for tt in range(4):
                ti = bi * 4 + tt
                p.dma("sp", [(G["Y"][ti * 128:(ti + 1) * 128, :], yacc[tt].t[:, :])], yacc[tt].b, reads=[yacc[tt].b], writes=[G["Y_b"]])
        p.flush()


def build_program(steps=None, with_peer=True):
    nc = bass.Bass("TRN2", target_bir_lowering=False)
    G = {}

    def din(name, shape):
        G[name] = nc.dram_tensor(name, list(shape), F32, kind="ExternalInput").ap()

    din("x", [T, D])
    din("mem", [256, D])
    din("w_in_a", [2, D, 2048])
    din("w_pool", [2, 4, 384, 384])
    din("s_pool_l", [2, 128, 12])
    din("w_in_b", [2, D, 5120])
    din("w_mem_kv", [4, D, 1024])
    din("w_o", [4, D, D])
    din("ln_g", [4, 2, D])
    din("ln_b", [4, 2, D])
    din("peer_wq", [4, D, D])
    din("keysT", [4, 128, 2048])
    if with_peer:
        din("peer_uT", [4, D, NEXP])
        din("peer_v", [4, NEXP, D])
    din("ident", [128, 128])
    din("ones", [128, 128])
    din("rc", [128, 4, 512])
    din("masks", [128, 2, 128])

    def dscr(name, shape, dt):
        G[name] = nc.dram_tensor(name, list(shape), dt, kind="Internal").ap()
        G[name + "_b"] = Buf(name, accum=True)

    G["X"], G["X_b"], G["XT"], G["XT_b"] = [], [], [], []
    for i in range(2):
        G["X"].append(nc.dram_tensor("X%d" % i, [T, D], F32, kind="Internal").ap())
        G["X_b"].append(Buf("X%d" % i, accum=True))
        G["XT"].append(nc.dram_tensor("XT%d" % i, [D, T], BF16, kind="Internal").ap())
        G["XT_b"].append(Buf("XT%d" % i, accum=True))
    dscr("MEMT", [D, 256], BF16)
    dscr("MIXT", [D, T], BF16)
    out = nc.dram_tensor("out", [T, D], F32, kind="ExternalOutput").ap()
    G["out"] = out

    dscr("SALL", [T, 2048], F32)
    dscr("QKT", [3072, T], BF16)
    dscr("V", [T, 1536], BF16)
    dscr("TN", [T, 16], F32)
    dscr("Y", [T, D], F32)
    if with_peer:
        dscr("UB", [NG, 128, 16, EG], BF16)
        dscr("VB", [NG, 128, 4, D], BF16)
    if steps is None:
        steps = [("init",)]
        for l in range(DEPTH):
            if l % 2 == 0:
                steps += [("A1", l, l // 2)]
            else:
                steps += [("B1", l, l // 2), ("B2", l)]
            steps += [("O", l), ("Q", l), ("PEER", l), ("F", l)]
        steps += [("out",)]

    with ExitStack() as es:
        p = Prog(nc, es)
        cur = 0
        for st in steps:
            k = st[0]
            if k == "init":
                phase_init(p, G)
                G["X"][0] = G["x"]
            elif k == "A1":
                phase_A1(p, G, st[1], st[2], cur)
            elif k == "B1":
                phase_B1(p, G, st[1], st[2], cur)
            elif k == "B2":
                phase_B2(p, G, st[1], cur)
            elif k == "O":
                phase_O(p, G, st[1], cur, 0)
                cur = 1 - cur
            elif k == "Q":
                phase_Q(p, G, st[1], cur)
            elif k == "PEER":
                phase_PEER(p, G, st[1], cur)
            elif k == "F":
                phase_O(p, G, st[1], cur, 1)
                cur = 1 - cur
            elif k == "out":
                with ExitStack() as es2:
                    tl = [p.tile(es2, "cp", [128, D], F32) for _ in range(4)]
                    for ti in range(NT):
                        t_ = tl[ti % 4]
                        p.dma("sp", [(t_.t[:, :], G["X"][cur][ti * 128:(ti + 1) * 128, :])], t_.b, reads=[G["X_b"][cur]], writes=[t_.b])
                        p.dma("sp", [(out[ti * 128:(ti + 1) * 128, :], t_.t[:, :])], t_.b, reads=[t_.b])
                    p.flush()
            elif k == "dump":
                name, shape, dt = st[1], st[2], st[3]
                src = G[name] if not isinstance(G[name], list) else G[name][st[4]]
                srcb = G[name + "_b"] if not isinstance(G[name + "_b"], list) else G[name + "_b"][st[4]]
                dbg = nc.dram_tensor("dbg", list(shape), dt, kind="ExternalOutput").ap()
                with ExitStack() as es2:
                    tl = [p.tile(es2, "cpd", [128, shape[1]], dt) for _ in range(2)]
                    for c in range(shape[0] // 128):
                        t_ = tl[c % 2]
                        p.dma("sp", [(t_.t[:, :], src[c * 128:(c + 1) * 128, :])], t_.b, reads=[srcb], writes=[t_.b])
                        p.dma("sp", [(dbg[c * 128:(c + 1) * 128, :], t_.t[:, :])], t_.b, reads=[t_.b])
                    p.flush()
    return nc


def host_consts():
    ident = np.eye(128, dtype=np.float32)
    ones = np.ones((128, 128), dtype=np.float32)
    rc = np.zeros((128, 4, 512), dtype=np.float32)
    t = np.arange(512)
    for g in range(4):
        w = 2 << g
        rc[:, g, :] = 1.0 / np.minimum(t + 1, w).astype(np.float32)
    k = np.arange(128)[:, None]
    q = np.arange(128)[None, :]
    masks = np.stack([(k >= q), (k <= q)], axis=1).astype(np.float32)
    return {"ident": ident, "ones": ones, "rc": rc, "masks": masks}


def host_layout(inputs):
    w = {}
    for k in ("w_in_a", "w_pool", "w_in_b", "w_mem_kv", "w_o", "ln_g", "ln_b", "peer_wq", "peer_v"):
        w[k] = np.ascontiguousarray(np.asarray(inputs[k], dtype=np.float32))
    sp = np.asarray(inputs["s_pool"], dtype=np.float32)
    w["s_pool_l"] = np.ascontiguousarray(sp.reshape(2, 12, 128).transpose(0, 2, 1))
    pk = np.asarray(inputs["peer_keys"], dtype=np.float32)
    w["keysT"] = np.ascontiguousarray(pk.transpose(0, 4, 1, 2, 3).reshape(4, 128, 2048))
    pu = np.asarray(inputs["peer_u"], dtype=np.float32)
    w["peer_uT"] = np.ascontiguousarray(pu.transpose(0, 2, 1))
    w.update(host_consts())
    return w


def kernel(**inputs):
    x = np.asarray(inputs["x"], dtype=np.float32)
    mem = np.asarray(inputs["mem"], dtype=np.float32)
    w = host_layout(inputs)
    nc = build_program()
    ncores = 8
    in_maps = []
    for c in range(ncores):
        m = dict(w)
        m["x"] = np.ascontiguousarray(x[c])
        m["mem"] = np.ascontiguousarray(mem[c])
        in_maps.append(m)
    res = run_bass_kernel_spmd(nc, in_maps, core_ids=list(range(ncores)))
    return np.stack([np.asarray(r["out"], dtype=np.float32) for r in res.results], axis=0)
```
